# Optimizing a Trainium2 kernel written in Bass

```python
import jax, jax.numpy as jnp
from jax import lax
import numpy as np

D_MODEL = 1024
BATCH = 8
SEQ = 4096
DEPTH = 1

N_ATT_HEADS = 8
ATT_HEAD_DIM = 64
ATT_WIDTH = N_ATT_HEADS * ATT_HEAD_DIM
KV_LATENT = 128
IDX_HEADS = 4
IDX_DIM = 64
TOPK_MAX = 256
Q_BLOCK = 128
N_MLP_GROUPS = 8
MLP_GROUP_DIM = 64
MLP_WIDTH = N_MLP_GROUPS * MLP_GROUP_DIM
CHUNK = 128
MIX_WIDTH = ATT_WIDTH + MLP_WIDTH
D_FF = 2816
DEEPNORM_ALPHA = float((2 * DEPTH) ** 0.25)
DEEPNORM_BETA = float((8 * DEPTH) ** -0.25)
LN_EPS = 1e-5
IN_COLS = ATT_WIDTH + KV_LATENT + IDX_HEADS * IDX_DIM + IDX_DIM + IDX_HEADS + 2 * MLP_WIDTH
SPLITS = tuple(np.cumsum([ATT_WIDTH, KV_LATENT, IDX_HEADS * IDX_DIM, IDX_DIM, IDX_HEADS]).tolist())

kernel_name = "hybrid_dsa_gmlp_macaron_deepnorm"


def _layernorm(x, g, b):
    xf = x.astype(jnp.float32)
    mu = jnp.mean(xf, axis=-1, keepdims=True)
    var = jnp.mean(jnp.square(xf - mu), axis=-1, keepdims=True)
    y = (xf - mu) * lax.rsqrt(var + LN_EPS)
    return (y * g.astype(jnp.float32) + b.astype(jnp.float32)).astype(x.dtype)


def _rmsnorm(x, g):
    xf = x.astype(jnp.float32)
    y = xf * lax.rsqrt(jnp.mean(jnp.square(xf), axis=-1, keepdims=True) + LN_EPS)
    return (y * g.astype(jnp.float32)).astype(x.dtype)


def _swiglu(x, w1, w3, w2):
    return (jax.nn.silu(x @ w1) * (x @ w3)) @ w2


def _sparse_attention(q, c, qi, ki, wi, w_uk, w_uv, topk):
    B, L = q.shape[0], q.shape[1]
    q_abs = jnp.einsum('blhd,hcd->blhc', q, w_uk) * (ATT_HEAD_DIM ** -0.5)
    wi_s = wi.astype(jnp.float32) * (IDX_HEADS ** -0.5)
    ki_f = ki.astype(jnp.float32)
    key_pos = jnp.arange(L)

    def block(i):
        start = i * Q_BLOCK
        qa = lax.dynamic_slice_in_dim(q_abs, start, Q_BLOCK, axis=1)
        qib = lax.dynamic_slice_in_dim(qi, start, Q_BLOCK, axis=1)
        wib = lax.dynamic_slice_in_dim(wi_s, start, Q_BLOCK, axis=1)
        t = start + jnp.arange(Q_BLOCK)
        causal = key_pos[None, :] <= t[:, None]
        logit = jnp.einsum('bqhd,bsd->bqhs', qib.astype(jnp.float32), ki_f) * (IDX_DIM ** -0.5)
        score = jnp.einsum('bqh,bqhs->bqs', wib, jax.nn.relu(logit))
        score = jnp.where(causal[None], score, -jnp.inf)
        _, sel = lax.top_k(score, topk)
        c_sel = jax.vmap(lambda cb, ib: cb[ib])(c, sel)
        valid = sel <= t[None, :, None]
        sc = jnp.einsum('bqhc,bqkc->bqhk', qa.astype(jnp.float32), c_sel.astype(jnp.float32))
        sc = jnp.where(valid[:, :, None, :], sc, -jnp.inf)
        p = jax.nn.softmax(sc, axis=-1).astype(c.dtype)
        o_lat = jnp.einsum('bqhk,bqkc->bqhc', p, c_sel)
        return jnp.einsum('bqhc,hcd->bqhd', o_lat, w_uv)

    out = lax.map(block, jnp.arange(L // Q_BLOCK))
    return out.transpose(1, 0, 2, 3, 4).reshape(B, L, ATT_WIDTH)


def _chunked_gmlp(z, gln_g, gln_b, ws, bs):
    B, L = z.shape[0], z.shape[1]
    z = jax.nn.gelu(z)
    u, v = jnp.split(z, 2, axis=-1)
    v = _layernorm(v, gln_g, gln_b)
    v = v.reshape(B, L // CHUNK, CHUNK, N_MLP_GROUPS, MLP_GROUP_DIM)
    tri = jnp.tril(jnp.ones((CHUNK, CHUNK), dtype=bool))
    ws_c = jnp.where(tri[None], ws, jnp.zeros_like(ws))
    sv = jnp.einsum('gts,bnsgd->bntgd', ws_c, v) + bs.T[None, None, :, :, None]
    return u * sv.reshape(B, L, MLP_WIDTH)


def _mixer(h, w_in, kv_norm_g, w_uk, w_uv, gln_g, gln_b, ws, bs, w_out, topk):
    B, L, _ = h.shape
    proj = h @ w_in
    q, c, qi, ki, wi, z = jnp.split(proj, SPLITS, axis=-1)
    q = q.reshape(B, L, N_ATT_HEADS, ATT_HEAD_DIM)
    c = _rmsnorm(c, kv_norm_g)
    qi = qi.reshape(B, L, IDX_HEADS, IDX_DIM)
    att = _sparse_attention(q, c, qi, ki, wi, w_uk, w_uv, topk)
    gm = _chunked_gmlp(z, gln_g, gln_b, ws, bs)
    return jnp.concatenate([att, gm], axis=-1) @ w_out


def setup_inputs(seed: int = 0) -> dict:
    key = jax.random.key(seed)
    ks = jax.random.split(key, 24)
    f32 = jnp.float32

    def nrm(k, shape, scale):
        return jax.random.normal(k, shape, f32) * scale

    def gain(k, shape):
        return 1.0 + 0.01 * jax.random.normal(k, shape, f32)

    L_ = DEPTH
    return {
        "x": jax.random.normal(ks[0], (BATCH, SEQ, D_MODEL), f32),
        "ffn1_w1": nrm(ks[1], (L_, D_MODEL, D_FF), D_MODEL ** -0.5),
        "ffn1_w3": nrm(ks[2], (L_, D_MODEL, D_FF), D_MODEL ** -0.5),
        "ffn1_w2": nrm(ks[3], (L_, D_FF, D_MODEL), DEEPNORM_BETA * D_FF ** -0.5),
        "ln1_g": gain(ks[4], (L_, D_MODEL)),
        "ln1_b": nrm(ks[5], (L_, D_MODEL), 0.01),
        "w_in": nrm(ks[6], (L_, D_MODEL, IN_COLS), D_MODEL ** -0.5),
        "kv_norm_g": gain(ks[7], (L_, KV_LATENT)),
        "w_uk": nrm(ks[8], (L_, N_ATT_HEADS, KV_LATENT, ATT_HEAD_DIM), KV_LATENT ** -0.5),
        "w_uv": nrm(ks[9], (L_, N_ATT_HEADS, KV_LATENT, ATT_HEAD_DIM), DEEPNORM_BETA * KV_LATENT ** -0.5),
        "gmlp_ln_g": gain(ks[10], (L_, MLP_WIDTH)),
        "gmlp_ln_b": nrm(ks[11], (L_, MLP_WIDTH), 0.01),
        "gmlp_ws": nrm(ks[12], (L_, N_MLP_GROUPS, CHUNK, CHUNK), 0.5 * CHUNK ** -0.5),
        "gmlp_bs": gain(ks[13], (L_, N_MLP_GROUPS, CHUNK)),
        "w_out": nrm(ks[14], (L_, MIX_WIDTH, D_MODEL), DEEPNORM_BETA * MIX_WIDTH ** -0.5),
        "ln2_g": gain(ks[15], (L_, D_MODEL)),
        "ln2_b": nrm(ks[16], (L_, D_MODEL), 0.01),
        "ffn2_w1": nrm(ks[17], (L_, D_MODEL, D_FF), D_MODEL ** -0.5),
        "ffn2_w3": nrm(ks[18], (L_, D_MODEL, D_FF), D_MODEL ** -0.5),
        "ffn2_w2": nrm(ks[19], (L_, D_FF, D_MODEL), DEEPNORM_BETA * D_FF ** -0.5),
        "ln3_g": gain(ks[20], (L_, D_MODEL)),
        "ln3_b": nrm(ks[21], (L_, D_MODEL), 0.01),
    }


def reference(x, ffn1_w1, ffn1_w3, ffn1_w2, ln1_g, ln1_b, w_in, kv_norm_g, w_uk, w_uv,
              gmlp_ln_g, gmlp_ln_b, gmlp_ws, gmlp_bs, w_out, ln2_g, ln2_b,
              ffn2_w1, ffn2_w3, ffn2_w2, ln3_g, ln3_b):
    L = x.shape[1]
    topk = min(TOPK_MAX, L // 4)
    for l in range(DEPTH):
        x = _layernorm(DEEPNORM_ALPHA * x + 0.5 * _swiglu(x, ffn1_w1[l], ffn1_w3[l], ffn1_w2[l]),
                       ln1_g[l], ln1_b[l])
        m = _mixer(x, w_in[l], kv_norm_g[l], w_uk[l], w_uv[l], gmlp_ln_g[l], gmlp_ln_b[l],
                   gmlp_ws[l], gmlp_bs[l], w_out[l], topk)
        x = _layernorm(DEEPNORM_ALPHA * x + m, ln2_g[l], ln2_b[l])
        x = _layernorm(DEEPNORM_ALPHA * x + 0.5 * _swiglu(x, ffn2_w1[l], ffn2_w3[l], ffn2_w2[l]),
                       ln3_g[l], ln3_b[l])
    return x
```

```python
from contextlib import ExitStack
import numpy as np
import concourse.bass as bass
import concourse.mybir as mybir
from concourse.bass_utils import run_bass_kernel_spmd

F32 = mybir.dt.float32
BF16 = mybir.dt.bfloat16
AF = mybir.ActivationFunctionType
ALU = mybir.AluOpType
AX = mybir.AxisListType

D = 1024
DFF = 2816
NFC = 22
NKC = 8
TM = 512
NTB = 4
LAT = 128
TOPK = 256
ALPHA = float(2.0 ** 0.25)
EPS = 1e-5
NIT = 22
NSLOT = 10
SLOT = 1152
TIE_EPS = 1.0e-9
NEG = -1.0e30


class _Op:
    __slots__ = ("eng", "fn", "waits", "awaited", "idx", "dsem", "dval", "val")


class Sched:
    ENGS = ("pe", "act", "dve", "pool", "sp")

    def __init__(self):
        self.ops = {e: [] for e in self.ENGS}
        self.state = {}
        self.seen = {e: {} for e in self.ENGS}
        self.dma_cnt = {}

    def emit(self, eng, fn, reads=(), writes=(), dsem=None, extra=()):
        op = _Op()
        op.eng = eng
        op.fn = fn
        op.awaited = False
        op.dsem = dsem
        op.dval = None
        op.val = None
        deps = []
        for k in reads:
            st = self.state.get(k)
            if st is not None and st[0] is not None:
                deps.append(st[0])
        for k in writes:
            st = self.state.get(k)
            if st is not None:
                if st[0] is not None:
                    deps.append(st[0])
                deps.extend(st[1])
        deps.extend(extra)
        seen = self.seen[eng]
        best = {}
        for d in deps:
            if d.dsem is not None:
                key = ("d", d.dsem)
                v = d.dval
            else:
                if d.eng == "pe" and eng == "pe":
                    continue
                key = ("e", d.eng)
                v = d.idx
            if v <= seen.get(key, -1):
                continue
            if key not in best or best[key][0] < v:
                best[key] = (v, d)
        waits = []
        for key, (v, d) in best.items():
            seen[key] = v
            d.awaited = True
            waits.append(d)
        op.waits = waits
        op.idx = len(self.ops[eng])
        self.ops[eng].append(op)
        if dsem is not None:
            c = self.dma_cnt.get(dsem, 0) + 16
            self.dma_cnt[dsem] = c
            op.dval = c
        for k in reads:
            st = self.state.setdefault(k, [None, []])
            st[1].append(op)
        for k in writes:
            self.state[k] = [op, []]
        return op

    def finalize(self):
        for e in self.ENGS:
            c = 0
            for op in self.ops[e]:
                if op.dsem is None and op.awaited:
                    c += 1
                    op.val = c


def build_nc(L):
    NMT = L // TM
    NBLK = L // 128
    nc = bass.Bass("TRN2", target_bir_lowering=False)
    S = Sched()
    es = ExitStack()

    def din(name, shape, dt=F32):
        return nc.dram_tensor(name, list(shape), dt, kind="ExternalInput").ap()

    def dscr(name, shape, dt=BF16):
        return nc.dram_tensor(name, list(shape), dt, kind="Internal").ap()

    x_d = din("x", [L, D])
    out_d = nc.dram_tensor("out", [L, D], F32, kind="ExternalOutput").ap()
    w1_d = [din("ffn1_w1", [D, DFF]), din("ffn2_w1", [D, DFF])]
    w3_d = [din("ffn1_w3", [D, DFF]), din("ffn2_w3", [D, DFF])]
    w2_d = [din("ffn1_w2", [DFF, D]), din("ffn2_w2", [DFF, D])]
    lng_d = [din("ln1_g", [1, D]), din("ln2_g", [1, D]), din("ln3_g", [1, D])]
    lnb_d = [din("ln1_b", [1, D]), din("ln2_b", [1, D]), din("ln3_b", [1, D])]
    win_d = din("w_in", [D, 1988])
    kvg_d = din("kv_norm_g", [1, LAT])
    wuk_d = din("w_uk", [8, LAT, 64])
    wuv_d = din("w_uv", [8, LAT, 64])
    glg_d = din("gmlp_ln_g", [1, 512])
    glb_d = din("gmlp_ln_b", [1, 512])
    ws_d = din("gmlp_ws", [8, 128, 128])
    bs_d = din("gmlp_bs", [8, 128])
    wout_d = din("w_out", [D, D])
    ident_d = din("c_ident", [128, 128])
    trineg_d = din("c_trineg", [128, 128])
    tril_d = din("c_tril", [128, 128])
    ramp_d = din("c_ramp", [1, 512])
    pow2_d = din("c_pow2", [1, 32])

    w1s = [dscr(f"w1s{f}", [NFC, 128, NKC * 128]) for f in range(2)]
    w3s = [dscr(f"w3s{f}", [NFC, 128, NKC * 128]) for f in range(2)]
    w2s = [dscr(f"w2s{f}", [NFC, 128, D]) for f in range(2)]
    winA = dscr("winA", [7, 128, NKC * 128])
    winZ = dscr("winZ", [NKC, 128, 1024])
    winC = dscr("winC", [128, NKC * 132])
    wouts = dscr("wouts", [NKC, 128, D])

    def sb(name, shape, dt=F32):
        return es.enter_context(nc.sbuf_tensor(name, list(shape), dt))

    x_res = sb("x_res", [128, NTB, D])
    u_sb2 = sb("u_sb", [128, 2, D])
    u_sb = u_sb2[:, 0, :]
    stg = u_sb2[:, 0, :].rearrange("p (a b) -> p a b", a=8)
    obuf = sb("obuf", [128, 2, D])
    xT = sb("xT", [128, NKC, TM], BF16)
    arena = sb("arena", [128, 12288], BF16)
    pool_sb = sb("wpool", [128, NSLOT, SLOT], BF16)
    lng_sb = sb("lng", [128, D])
    lnb_sb = sb("lnb", [128, D])
    glg_sb = sb("glg", [128, 512])
    glb_sb = sb("glb", [128, 512])
    kvg_sb = sb("kvg", [128, LAT])
    cT = sb("cT", [128, L], BF16)
    c_tok = sb("c_tok", [128, NBLK, LAT], BF16)
    kiT = sb("kiT", [128, L], BF16)
    rtmp = sb("rtmp", [128, 2, 512])
    maskT = sb("maskT", [128, 2, NBLK, 128], BF16)
    qT = sb("qT", [128, 4, TM], BF16)
    qabsT = sb("qabsT", [128, 8, TM], BF16)
    qiT = sb("qiT", [128, 2, TM], BF16)
    w_tok = sb("w_tok", [128, NTB, 4])
    ug = sb("ug", [128, 512])
    vg = sb("vg", [128, 512])
    vn = sb("vn", [128, 512], BF16)
    gm_tok = sb("gm_tok", [128, 512], BF16)
    gmT = sb("gmT", [128, 4, TM], BF16)
    attT = sb("attT", [128, 4, TM], BF16)
    PT = sb("PT", [128, 3, 512], BF16)
    rden = sb("rden", [128, 512])
    sgb = sb("sgb", [128, 2, 512], BF16)
    olat = sb("olat", [128, 2, 512], BF16)
    wukT = sb("wukT", [128, 4, 128], BF16)
    wuvp = sb("wuvp", [128, 8, 128], BF16)
    wsT = sb("wsT", [128, 8, 128], BF16)
    bs_tok = sb("bs_tok", [128, 8])
    ident_f = sb("ident_f", [128, 128])
    ident_b = sb("ident_b", [128, 128], BF16)
    ones_b = sb("ones_b", [128, 128], BF16)
    trineg = sb("trineg", [128, 128])
    tril = sb("tril", [128, 128])
    ramp = sb("ramp", [128, 512])
    stats = sb("stats", [128, 2, 6])
    mv = sb("mv", [128, 2])
    sm = sb("sm", [128, 16])
    lstats = sb("lstats", [128, 2, 12])
    lmv = sb("lmv", [128, 2, 2])
    lsm = sb("lsm", [128, 2, 4])
    bis = sb("bis", [128, 8])
    p2t = sb("p2t", [128, 32])
    wtab = sb("wtab", [128, 32])
    w2tab = sb("w2tab", [128, 32])
    midt = sb("midt", [128, 32])
    cntt = sb("cntt", [128, 32])
    sgnt = sb("sgnt", [128, 32])
    c2t = sb("c2t", [128, 32])
    at = sb("at", [128, 32])
    mmt = sb("mmt", [128, 32])
    lnd = sb("lnd", [128, 512])
    junk = sb("junk", [128, 128])

    ps = [es.enter_context(nc.psum_tensor(f"ps{i}", [128, 512], F32)) for i in range(8)]
    psb = [p.bitcast(BF16) for p in ps]

    gT = arena
    scores = arena.bitcast(F32)
    maskA = arena

    def gTs(fc, lo, hi):
        return gT[:, fc * TM + lo: fc * TM + hi]

    E = S.emit

    def dma(q, out, in_, reads, writes, dsem, extra=()):
        return E(q, lambda e: e.dma_start(out=out, in_=in_), reads, writes, dsem=dsem, extra=extra)

    def mm(out, lhsT, rhs, start, stop, reads, writes):
        return E("pe", lambda e: e.matmul(out, lhsT, rhs, start=start, stop=stop), reads, writes)

    def tr(out, in_, ident, reads, writes):
        return E("pe", lambda e: e.transpose(out, in_, ident), reads, writes)

    def act(out, in_, func, reads, writes, bias=None, scale=None, accum_out=None):
        kw = {}
        if bias is not None:
            kw["bias"] = bias
        if scale is not None:
            kw["scale"] = scale
        if accum_out is not None:
            kw["accum_out"] = accum_out
        return E("act", lambda e: e.activation(out=out, in_=in_, func=func, **kw), reads, writes)

    def ts(eng, out, in0, s1, s2, op0, op1, reads, writes, accum_out=None):
        if op1 is None:
            return E(eng, lambda e: e.tensor_scalar(out, in0, s1, None, op0), reads, writes)
        if accum_out is not None:
            return E(eng, lambda e: e.tensor_scalar(out, in0, s1, s2, op0, op1, accum_out=accum_out), reads, writes)
        return E(eng, lambda e: e.tensor_scalar(out, in0, s1, s2, op0, op1), reads, writes)

    def tt(eng, out, in0, in1, op, reads, writes):
        return E(eng, lambda e: e.tensor_tensor(out, in0, in1, op), reads, writes)

    def stt(out, in0, scalar, in1, op0, op1, reads, writes):
        return E("dve", lambda e: e.scalar_tensor_tensor(out, in0, scalar, in1, op0, op1), reads, writes)

    def cp(eng, out, in_, reads, writes):
        if eng == "act":
            return E("act", lambda e: e.copy(out=out, in_=in_), reads, writes)
        return E(eng, lambda e: e.tensor_copy(out, in_), reads, writes)

    def bc(ap1, n):
        return bass.AP(ap1.tensor, 0, [[0, 128], [1, n]])

    setup_ops = []

    def setup_dma(out, in_, key):
        op = dma("pool", out, in_, [], [key], dsem="setup")
        setup_ops.append(op)
        return op

    setup_dma(ident_f[:], ident_d, "ident_f")
    setup_dma(trineg[:], trineg_d, "trineg")
    setup_dma(tril[:], tril_d, "tril")
    setup_dma(ramp[:], bc(ramp_d, 512), "ramp")
    setup_dma(p2t[:], bc(pow2_d, 32), "p2t")
    setup_dma(glg_sb[:], bc(glg_d, 512), "glg")
    setup_dma(glb_sb[:], bc(glb_d, 512), "glb")
    setup_dma(kvg_sb[:], bc(kvg_d, LAT), "kvg")
    setup_dma(stg[:, :, 0:64], wuk_d.rearrange("h c d -> c h d"), "u_sb")
    tot = S.dma_cnt["setup"]
    for op in setup_ops:
        op.dval = tot

    E("dve", lambda e: e.tensor_copy(ident_b[:], ident_f[:]), ["ident_f"], ["ident_b"])
    E("pool", lambda e: e.memset(ones_b[:], 1.0), [], ["ones_b"])
    E("pool", lambda e: e.memset(wuvp[:], 0.0), [], ["wuvp"])
    for h in range(0, 8, 2):
        b = ps[(h // 2) % 2]
        key = f"ps{(h // 2) % 2}"
        tr(b[0:64, 0:128], stg[:, h, 0:64], ident_f[:], ["u_sb", "ident_f"], [key])
        cp("dve", wukT[0:64, h // 2, :], b[0:64, 0:128], [key], ["wukT"])
    E("pool", lambda e: e.memset(junk[:], 0.0), [], ["junk"])
    for h in range(1, 8, 2):
        cp("dve", junk[:, 64:128], stg[:, h, 0:64], ["u_sb"], ["junk"])
        b = ps[2 + (h // 2) % 2]
        key = f"ps{2 + (h // 2) % 2}"
        tr(b[:, 0:128], junk[:], ident_f[:], ["junk", "ident_f"], [key])
        cp("dve", wukT[64:128, h // 2, :], b[64:128, 0:128], [key], ["wukT"])
    op_uv = dma("pool", stg[:, :, 0:64], wuv_d.rearrange("h c d -> c h d"), [], ["u_sb"], dsem="setup2")
    for h in range(8):
        off = (h % 2) * 64
        cp("dve", wuvp[:, h, off:off + 64], stg[:, h, 0:64], ["u_sb"], ["wuvp"])
    op_ws = dma("pool", stg[:], ws_d.rearrange("g t s -> t g s"), [], ["u_sb"], dsem="setup2")
    for g in range(8):
        b = ps[4 + g % 2]
        key = f"ps{4 + g % 2}"
        tr(b[:, 0:128], stg[:, g, :], ident_f[:], ["u_sb", "ident_f"], [key])
        tt("dve", wsT[:, g, :], b[:, 0:128], tril[:], ALU.mult, [key, "tril"], ["wsT"])
    op_bs = dma("pool", stg[0:8, 0, :], bs_d, [], ["u_sb"], dsem="setup2")
    tr(ps[6][:, 0:8], stg[0:8, 0, :], ident_f[0:8, 0:8], ["u_sb", "ident_f"], ["ps6"])
    cp("dve", bs_tok[:], ps[6][:, 0:8], ["ps6"], ["bs_tok"])

    cast_hist = []

    def cast(out, in_, key):
        n = len(cast_hist)
        extra = [cast_hist[n - 8]] if n >= 8 else []
        op = dma("pool", out, in_, [], [key], dsem=f"cast{n % 8}", extra=extra)
        cast_hist.append(op)

    def cast_ffn_up(f):
        for fc in range(NFC):
            cast(w1s[f][fc].rearrange("p (kc f) -> p kc f", kc=NKC),
                 w1_d[f][:, fc * 128:(fc + 1) * 128].rearrange("(kc p) f -> p kc f", p=128), f"w1s{f}.{fc}")
            cast(w3s[f][fc].rearrange("p (kc f) -> p kc f", kc=NKC),
                 w3_d[f][:, fc * 128:(fc + 1) * 128].rearrange("(kc p) f -> p kc f", p=128), f"w3s{f}.{fc}")

    def cast_ffn_down(f):
        for fc in range(NFC):
            cast(w2s[f][fc], w2_d[f][fc * 128:(fc + 1) * 128, :], f"w2s{f}.{fc}")

    for tb in range(NTB):
        r0_ = tb * 128
        dma("pool", x_res[:, tb, :], x_d[r0_:r0_ + 128, :], [], [f"x_res{tb}"], dsem=f"xin{tb}")
    dma("pool", lng_sb[:], bc(lng_d[0], D), [], ["lng"], dsem="lng")
    dma("pool", lnb_sb[:], bc(lnb_d[0], D), [], ["lnb"], dsem="lnb")
    cast_ffn_up(0)
    cast_ffn_down(0)
    colsA = [(0, 128), (128, 256), (256, 384), (384, 512), (640, 768), (768, 896)]
    for ci, (a, b_) in enumerate(colsA):
        cast(winA[ci].rearrange("p (kc f) -> p kc f", kc=NKC),
             win_d[:, a:b_].rearrange("(kc p) f -> p kc f", p=128), f"winA.{ci}")
    for half in range(2):
        cast(winA[6].rearrange("p (kc f) -> p kc f", kc=NKC)[:, :, half * 64:(half + 1) * 64],
             win_d[:, 896:960].rearrange("(kc p) f -> p kc f", p=128), f"winA.6.{half}")
    for kc in range(NKC):
        cast(winZ[kc], win_d[kc * 128:(kc + 1) * 128, 964:1988], f"winZ.{kc}")
    cast(winC.rearrange("p (kc f) -> p kc f", kc=NKC)[:, :, 0:128],
         win_d[:, 512:640].rearrange("(kc p) f -> p kc f", p=128), "winC.a")
    cast(winC.rearrange("p (kc f) -> p kc f", kc=NKC)[:, :, 128:132],
         win_d[:, 960:964].rearrange("(kc p) f -> p kc f", p=128), "winC.b")
    for kc in range(NKC):
        cast(wouts[kc], wout_d[kc * 128:(kc + 1) * 128, :], f"wouts.{kc}")

    chunks = []
    for mt in range(NMT):
        for f in range(2):
            if f == 1:
                for ci in range(7):
                    keys = [f"winA.{ci}"] if ci < 6 else ["winA.6.0", "winA.6.1"]
                    chunks.append((winA[ci], keys, 1024))
                for kc in range(NKC):
                    chunks.append((winZ[kc], [f"winZ.{kc}"], 1024))
                chunks.append((winC, ["winC.a", "winC.b"], NKC * 132))
                for kc in range(NKC):
                    chunks.append((wouts[kc], [f"wouts.{kc}"], 1024))
            for fc in range(NFC):
                chunks.append((w1s[f][fc], [f"w1s{f}.{fc}"], 1024))
                chunks.append((w3s[f][fc], [f"w3s{f}.{fc}"], 1024))
            for fc in range(NFC):
                chunks.append((w2s[f][fc], [f"w2s{f}.{fc}"], 1024))
    stream = {"next_load": 0, "next_use": 0}

    def load_chunk():
        n = stream["next_load"]
        if n >= len(chunks):
            return
        src, keys, ne = chunks[n]
        s = n % NSLOT
        dma("sp", pool_sb[:, s, 0:ne], src, keys, [f"slot{s}"], dsem=f"slot{s}")
        stream["next_load"] = n + 1

    def acquire():
        n = stream["next_use"]
        stream["next_use"] = n + 1
        s = n % NSLOT
        return s, f"slot{s}"

    def release():
        load_chunk()

    for _ in range(NSLOT):
        load_chunk()

    def load_x_block(mt, tb):
        r0 = mt * TM + tb * 128
        dma("pool", x_res[:, tb, :], x_d[r0:r0 + 128, :], [], [f"x_res{tb}"], dsem=f"xin{tb}")

    def make_xT_block(tb):
        tsl = slice(tb * 128, (tb + 1) * 128)
        for half in range(2):
            b = half
            for k4 in range(4):
                kc = half * 4 + k4
                tr(ps[b][:, k4 * 128:(k4 + 1) * 128], x_res[:, tb, kc * 128:(kc + 1) * 128], ident_f[:],
                   [f"x_res{tb}", "ident_f"], [f"ps{b}"])
            keys = [f"xT{half * 4 + k4}" for k4 in range(4)]
            dst = xT[:, half * 4:half * 4 + 4, tsl]
            src = ps[b][:, :].rearrange("p (k t) -> p k t", k=4)
            cp("act", dst, src, [f"ps{b}"], keys)

    def make_xT():
        for tb in range(NTB):
            make_xT_block(tb)

    xT_keys = [f"xT{kc}" for kc in range(NKC)]

    def ffn(f):
        for fc in range(NFC):
            s1, k1 = acquire()
            s3, k3 = acquire()
            b1 = 2 + (fc % 2) * 2
            b3 = b1 + 1
            for kc in range(NKC):
                mm(ps[b1][:, :], pool_sb[:, s1, kc * 128:(kc + 1) * 128], xT[:, kc, :], kc == 0, kc == NKC - 1,
                   [k1, f"xT{kc}"], [f"ps{b1}"])
            for kc in range(NKC):
                mm(ps[b3][:, :], pool_sb[:, s3, kc * 128:(kc + 1) * 128], xT[:, kc, :], kc == 0, kc == NKC - 1,
                   [k3, f"xT{kc}"], [f"ps{b3}"])
            release()
            release()
            sg = sgb[:, fc % 2, :]
            act(sg, ps[b1][:, :], AF.Silu, [f"ps{b1}"], [f"sg{fc % 2}"])
            stt(gTs(fc, 0, TM), ps[b3][:, :], 0.5, sg, ALU.mult, ALU.mult, [f"ps{b3}", f"sg{fc % 2}"], ["arena", "mkA", "mkB"])
        for fc in range(NFC):
            s2, k2 = acquire()
            for tb in range(NTB):
                for dh in range(2):
                    b = tb * 2 + dh
                    mm(ps[b][:, :], gTs(fc, tb * 128, (tb + 1) * 128), pool_sb[:, s2, dh * 512:(dh + 1) * 512],
                       fc == 0, fc == NFC - 1, [k2, "arena"], [f"ps{b}"])
            release()

    def load_ln(i):
        dma("pool", lng_sb[:], bc(lng_d[i], D), [], ["lng"], dsem="lng")
        dma("pool", lnb_sb[:], bc(lnb_d[i], D), [], ["lnb"], dsem="lnb")

    def ln_front(tb, banks):
        par = tb % 2
        ub = u_sb2[:, par, :]
        uk = f"u_sb{par}" if par == 1 else "u_sb"
        for dh in range(2):
            b = banks[dh]
            stt(u_sb2[:, par, dh * 512:(dh + 1) * 512], x_res[:, tb, dh * 512:(dh + 1) * 512], ALPHA, ps[b][:, :],
                ALU.mult, ALU.add, [f"x_res{tb}", f"ps{b}"], [uk])
        for dh in range(2):
            E("dve", lambda e, dh=dh: e.bn_stats(lstats[:, par, dh * 6:(dh + 1) * 6],
                                                 u_sb2[:, par, dh * 512:(dh + 1) * 512]), [uk], [f"lstats{par}"])
        E("dve", lambda e: e.bn_aggr(lmv[:, par, :], lstats[:, par, :]), [f"lstats{par}"], [f"lmv{par}"])
        ts("dve", lsm[:, par, 0:1], lmv[:, par, 1:2], EPS, None, ALU.add, None, [f"lmv{par}"], [f"lsm0{par}"])
        act(lsm[:, par, 1:2], lsm[:, par, 0:1], AF.Sqrt, [f"lsm0{par}"], [f"lsm1{par}"])
        E("dve", lambda e: e.reciprocal(lsm[:, par, 2:3], lsm[:, par, 1:2]), [f"lsm1{par}"], [f"lsm2{par}"])
        stt(lsm[:, par, 3:4], lmv[:, par, 0:1], -1.0, lsm[:, par, 2:3], ALU.mult, ALU.mult,
            [f"lmv{par}", f"lsm2{par}"], [f"lsm3{par}"])
        act(ub, ub, AF.Identity, [uk, f"lsm2{par}", f"lsm3{par}"], [uk], bias=lsm[:, par, 3:4], scale=lsm[:, par, 2:3])

    def ln_back(tb, dest, dest_key):
        par = tb % 2
        ub = u_sb2[:, par, :]
        uk = f"u_sb{par}" if par == 1 else "u_sb"
        tt("dve", dest, ub, lng_sb[:], ALU.mult, [uk, "lng"], [dest_key])
        tt("pool", dest, dest, lnb_sb[:], ALU.add, [dest_key, "lnb"], [dest_key])

    def layernorm_block(tb, banks, dest, dest_key):
        ln_front(tb, banks)
        ln_back(tb, dest, dest_key)

    def layernorm_tile(bank_fn, dest_fn, after_fn):
        def back(tb):
            d, dk = dest_fn(tb)
            ln_back(tb, d, dk)
        ln_front(0, bank_fn(0))
        ln_front(1, bank_fn(1))
        back(0)
        ln_front(2, bank_fn(2))
        after_fn(0)
        back(1)
        ln_front(3, bank_fn(3))
        after_fn(1)
        back(2)
        after_fn(2)
        back(3)
        after_fn(3)

    def proj_feature_major(mt):
        for ci in range(7):
            s, k = acquire()
            b = ci % 2
            for kc in range(NKC):
                mm(ps[b][:, :], pool_sb[:, s, kc * 128:(kc + 1) * 128], xT[:, kc, :], kc == 0, kc == NKC - 1,
                   [k, f"xT{kc}"], [f"ps{b}"])
            release()
            if ci < 4:
                cp("act", qT[:, ci, :], ps[b][:, :], [f"ps{b}"], ["qT"])
            elif ci < 6:
                cp("dve", qiT[:, ci - 4, :], ps[b][:, :], [f"ps{b}"], ["qiT"])
            else:
                cp("act", kiT[:, mt * TM:(mt + 1) * TM], ps[b][:, :], [f"ps{b}"], ["kiT"])
        for h in range(8):
            b = 2 + h % 2
            p0 = (h % 2) * 64
            mm(ps[b][:, :], wukT[p0:p0 + 64, h // 2, :], qT[p0:p0 + 64, h // 2, :], True, True,
               ["wukT", "qT"], [f"ps{b}"])
            if h % 2 == 0:
                E("act", lambda e, h=h, b=b: e.activation(out=qabsT[:, h, :], in_=ps[b][:, :], func=AF.Identity, scale=0.125),
                  [f"ps{b}"], ["qabsT"])
            else:
                ts("dve", qabsT[:, h, :], ps[b][:, :], 0.125, None, ALU.mult, None, [f"ps{b}"], ["qabsT"])

    def ptm_gen(mt):
        zs = []
        for kc in range(NKC):
            zs.append(acquire())
        sc_, kc_ = acquire()

        def zbanks(tb):
            return (4, 5) if tb % 2 == 0 else (2, 3)

        def mm_part(tb):
            tsl = slice(tb * 128, (tb + 1) * 128)
            zb = zbanks(tb)
            for half in range(2):
                b = zb[half]
                for kc in range(NKC):
                    s, k = zs[kc]
                    mm(ps[b][:, :], xT[:, kc, tsl], pool_sb[:, s, half * 512:(half + 1) * 512], kc == 0, kc == NKC - 1,
                       [k, f"xT{kc}"], [f"ps{b}"])
            for kc in range(NKC):
                mm(ps[6][:, 0:132], xT[:, kc, tsl], pool_sb[:, sc_, kc * 132:(kc + 1) * 132], kc == 0, kc == NKC - 1,
                   [kc_, f"xT{kc}"], ["ps6"])

        def elem_part(tb):
            blk = mt * NTB + tb
            zb = zbanks(tb)
            act(junk[:], ps[6][:, 0:128], AF.Square, ["ps6"], ["junk", "sm4"], accum_out=sm[:, 4:5])
            ts("dve", sm[:, 5:6], sm[:, 4:5], 1.0 / LAT, EPS, ALU.mult, ALU.add, ["sm4"], ["sm5"])
            act(sm[:, 6:7], sm[:, 5:6], AF.Sqrt, ["sm5"], ["sm6"])
            E("dve", lambda e: e.reciprocal(sm[:, 7:8], sm[:, 6:7]), ["sm6"], ["sm7"])
            stt(c_tok[:, blk, :], ps[6][:, 0:128], sm[:, 7:8], kvg_sb[:], ALU.mult, ALU.mult,
                ["ps6", "sm7", "kvg"], [f"c_tok{blk}"])
            cp("dve", w_tok[:, tb, :], ps[6][:, 128:132], ["ps6"], ["w_tok"])
            act(ug[:], ps[zb[0]][:, :], AF.Gelu_apprx_tanh, [f"ps{zb[0]}"], ["ug"])
            act(vg[:], ps[zb[1]][:, :], AF.Gelu_apprx_tanh, [f"ps{zb[1]}"], ["vg"])
            E("dve", lambda e: e.bn_stats(stats[:, 0, :], vg[:]), ["vg"], ["stats"])
            E("dve", lambda e: e.bn_aggr(mv[:], stats[:, 0, :]), ["stats"], ["mv"])
            ts("dve", sm[:, 8:9], mv[:, 1:2], EPS, None, ALU.add, None, ["mv"], ["sm8"])
            act(sm[:, 9:10], sm[:, 8:9], AF.Sqrt, ["sm8"], ["sm9"])
            E("dve", lambda e: e.reciprocal(sm[:, 10:11], sm[:, 9:10]), ["sm9"], ["sm10"])
            stt(sm[:, 11:12], mv[:, 0:1], -1.0, sm[:, 10:11], ALU.mult, ALU.mult, ["mv", "sm10"], ["sm11"])
            act(vg[:], vg[:], AF.Identity, ["vg", "sm10", "sm11"], ["vg"], bias=sm[:, 11:12], scale=sm[:, 10:11])
            tt("dve", vg[:], vg[:], glg_sb[:], ALU.mult, ["vg", "glg"], ["vg"])
            tt("pool", vn[:], vg[:], glb_sb[:], ALU.add, ["vg", "glb"], ["vn"])

        def spatial_part(tb):
            blk = mt * NTB + tb
            tsl = slice(tb * 128, (tb + 1) * 128)
            zb = zbanks(tb)
            tr(psb[7][:, 0:128], c_tok[:, blk, :], ident_b[:], [f"c_tok{blk}", "ident_b"], ["ps7"])
            cp("act", cT[:, blk * 128:(blk + 1) * 128], psb[7][:, 0:128], ["ps7"], ["cT"])
            for g in range(8):
                mm(ps[zb[0]][:, g * 64:(g + 1) * 64], wsT[:, g, :], vn[:, g * 64:(g + 1) * 64], True, True,
                   ["wsT", "vn"], [f"ps{zb[0]}"])
            tt("dve", vg[:].rearrange("p (g d) -> p g d", g=8), ps[zb[0]][:, :].rearrange("p (g d) -> p g d", g=8),
               bs_tok[:].unsqueeze(2).to_broadcast([128, 8, 64]), ALU.add, [f"ps{zb[0]}", "bs_tok"], ["vg"])
            tt("dve", gm_tok[:], vg[:], ug[:], ALU.mult, ["vg", "ug"], ["gm_tok"])
            for ch in range(4):
                tr(psb[7][:, 256 + ch * 128:256 + (ch + 1) * 128], gm_tok[:, ch * 128:(ch + 1) * 128], ident_b[:],
                   ["gm_tok", "ident_b"], ["ps7"])
            cp("act", gmT[:, :, tsl], psb[7][:, 256:768].rearrange("p (c t) -> p c t", c=4), ["ps7"], ["gmT"])

        mm_part(0)
        for tb in range(NTB):
            elem_part(tb)
            if tb + 1 < NTB:
                mm_part(tb + 1)
            yield ("elem", tb)
            spatial_part(tb)
            yield ("spatial", tb)
        for _ in range(NKC + 1):
            release()

    def indexer_a(mt, tb, bg=None, nsteps=0, deferred=None, is_attn_bg=False):
        i = mt * NTB + tb
        pulled0 = [0]
        n = 128 * (i + 1)
        tsl = slice(tb * 128, (tb + 1) * 128)
        nch = (n + 511) // 512
        for c5 in range(nch):
            k0 = c5 * 512
            kn = min(512, n - k0)
            for h in range(4):
                b = h % 2
                p0 = (h % 2) * 64
                mm(ps[b][:, 0:kn], qiT[p0:p0 + 64, h // 2, tsl], kiT[p0:p0 + 64, k0:k0 + kn], True, True,
                   ["qiT", "kiT"], [f"ps{b}"])
                act(rtmp[:, b, 0:kn], ps[b][:, 0:kn], AF.Relu, [f"ps{b}"], [f"rtmp{b}"])
                in1 = ramp[:, 0:kn] if h == 0 else scores[:, k0:k0 + kn]
                rk = ["ramp"] if h == 0 else []
                stt(scores[:, k0:k0 + kn], rtmp[:, b, 0:kn], w_tok[:, tb, h:h + 1], in1, ALU.mult, ALU.add,
                    [f"rtmp{b}", "w_tok"] + rk, ["arena"])
                if bg is not None and is_attn_bg and (h % 2 == 1) and pulled0[0] < nsteps - 1:
                    next(bg, None)
                    pulled0[0] += 1
            if c5 > 0:
                ts("dve", scores[:, k0:k0 + kn], scores[:, k0:k0 + kn], float(-TIE_EPS * k0), None, ALU.add, None,
                   [], ["arena"])
        tt("dve", scores[:, n - 128:n], scores[:, n - 128:n], trineg[:], ALU.add, ["trineg"], ["arena"])
        if deferred is not None:
            deferred()
        lo, mx, wd, av = (bis[:, 0:1], bis[:, 1:2], bis[:, 2:3], bis[:, 4:5])
        mk = maskA[:, 8192:8192 + n]
        if i < 2:
            E("dve", lambda e: e.memset(lo, -1.0e29), [], ["bis"])
        else:
            E("dve", lambda e: e.tensor_reduce(lo, scores[:, 0:n - 128], AX.X, ALU.min), ["arena"], ["bis"])
            E("dve", lambda e: e.tensor_reduce(mx, scores[:, 0:n], AX.X, ALU.max), ["arena"], ["bis"])
            tt("dve", wd, mx, lo, ALU.subtract, ["bis"], ["bis"])
            ts("dve", wd, wd, 1.0001, 1.0e-6, ALU.mult, ALU.add, ["bis"], ["bis"])
            ts("dve", wtab[:, 0:NIT + 1], p2t[:, 0:NIT + 1], wd, None, ALU.mult, None, ["bis", "p2t"], ["wtab"])
            ts("dve", w2tab[:, 0:NIT + 1], wtab[:, 0:NIT + 1], 2.0, None, ALU.mult, None, ["wtab"], ["w2tab"])
            tt("dve", midt[:, 0:1], lo, wtab[:, 0:1], ALU.add, ["bis", "wtab"], ["midt"])
            if bg is None:
                nsteps = 0
            rem = max(0, nsteps - pulled0[0])
            r = -(-rem // NIT) if rem else 0
            n1 = ((224.0 + n) / 1.2 + 530.0 * r + 250.0 - 600.0) / (1.0 / 0.88 + 1.0 / 1.2)
            n1 = int(min(max(64, round(n1 / 64.0) * 64), n - 128))
            n2 = n - n1
            thr = float(TOPK) - 0.5 - 0.5 * n2
            pulled = pulled0[0]
            E("dve", lambda e: e.memset(at[:, 0:1], 0.0), [], ["at"])
            E("dve", lambda e: e.tensor_copy(mmt[:, 0:1], midt[:, 0:1]), ["midt"], ["mmt"])
            for k in range(NIT):
                E("dve", lambda e, k=k: e.scalar_tensor_tensor(
                    mk[:, 0:n1], scores[:, 0:n1], mmt[:, k:k + 1], at[:, k:k + 1].to_broadcast([128, n1]),
                    ALU.subtract, ALU.is_ge, accum_out=cntt[:, k:k + 1]),
                  ["mmt", "at", "arena"], ["mkA", "cntt"])
                act(mk[:, n1:n], scores[:, n1:n], AF.Sign, ["midt", "arena"], ["mkB", "sgnt"],
                    bias=midt[:, k:k + 1], scale=-1.0, accum_out=sgnt[:, k:k + 1])
                if bg is not None:
                    for _ in range(r):
                        if pulled < nsteps - 1:
                            next(bg, None)
                            pulled += 1
                stt(mmt[:, k + 1:k + 2], at[:, k:k + 1], mmt[:, k:k + 1], wtab[:, k + 1:k + 2], ALU.add, ALU.subtract,
                    ["at", "mmt", "wtab"], ["mmt"])
                stt(c2t[:, k:k + 1], sgnt[:, k:k + 1], -0.5, cntt[:, k:k + 1], ALU.mult, ALU.add,
                    ["sgnt", "cntt"], ["c2t"])
                stt(at[:, k + 1:k + 2], c2t[:, k:k + 1], thr, w2tab[:, k + 1:k + 2], ALU.is_ge, ALU.mult,
                    ["c2t", "w2tab"], ["at"])
                act(midt[:, k + 1:k + 2], at[:, k + 1:k + 2], AF.Identity, ["mmt", "at"], ["midt"],
                    bias=mmt[:, k + 1:k + 2], scale=1.0)
            tt("dve", lo, midt[:, NIT:NIT + 1], wtab[:, NIT:NIT + 1], ALU.subtract, ["midt", "wtab"], ["bis"])
        ts("dve", mk, scores[:, 0:n], lo, None, ALU.is_ge, None, ["bis", "arena"], ["mkA", "mkB"])

    def indexer_b(mt, tb):
        i = mt * NTB + tb
        par = tb % 2
        nb = i + 1
        for gi, j0 in enumerate(range(0, nb, 4)):
            jn = min(4, nb - j0)
            bb = gi % 2
            for jj in range(jn):
                j = j0 + jj
                tr(psb[bb][:, jj * 128:(jj + 1) * 128], maskA[:, 8192 + j * 128:8192 + (j + 1) * 128], ident_b[:],
                   ["mkA", "mkB", "ident_b"], [f"ps{bb}"])
            act(maskT[:, par, j0:j0 + jn, :], psb[bb][:, 0:jn * 128].rearrange("p (j t) -> p j t", j=jn),
                AF.Identity, [f"ps{bb}"], [f"maskT{par}"], bias=-30000.0, scale=30000.0)

    def attention_gen(mt, tb):
        i = mt * NTB + tb
        par = tb % 2
        tsl = slice(tb * 128, (tb + 1) * 128)
        steps = [(hg, j) for hg in range(2) for j in range(i + 1)]

        def emit_S(idx):
            hg, j = steps[idx]
            bs_ = 6 + (idx % 2)
            mm(ps[bs_][:, :].rearrange("p (h t) -> p h t", h=4), cT[:, j * 128:(j + 1) * 128],
               qabsT[:, hg * 4:(hg + 1) * 4, tsl], True, False, ["cT", "qabsT"], [f"ps{bs_}"])
            mm(ps[bs_][:, :].rearrange("p (h t) -> p h t", h=4), ident_b[:],
               maskT[:, par, j, :].unsqueeze(1).to_broadcast([128, 4, 128]), False, True,
               ["ident_b", f"maskT{par}"], [f"ps{bs_}"])

        emit_S(0)
        if len(steps) > 1:
            emit_S(1)
        for idx, (hg, j) in enumerate(steps):
            bo, bd = (4, 5) if hg == 0 else (2, 3)
            bs_ = 6 + (idx % 2)
            pt = idx % 3
            act(PT[:, pt, :], ps[bs_][:, :], AF.Exp, [f"ps{bs_}"], [f"PT{pt}"])
            if idx + 2 < len(steps):
                emit_S(idx + 2)
            mm(ps[bo][:, :], c_tok[:, j, :], PT[:, pt, :], j == 0, j == i, [f"c_tok{j}", f"PT{pt}"], [f"ps{bo}"])
            mm(ps[bd][:, :], ones_b[:], PT[:, pt, :], j == 0, j == i, ["ones_b", f"PT{pt}"], [f"ps{bd}"])
            yield idx
        for hg in range(2):
            bo, bd = (4, 5) if hg == 0 else (2, 3)
            act(lnd[:], ps[bd][:, :], AF.Ln, [f"ps{bd}"], ["lnd"])
            act(rden[:], lnd[:], AF.Exp, ["lnd"], ["rden"], scale=-1.0)
            tt("dve", olat[:, hg, :], ps[bo][:, :], rden[:], ALU.mult, [f"ps{bo}", "rden"], [f"olat{hg}"])
            for hp in range(2):
                for e2 in range(2):
                    hl = hp * 2 + e2
                    h = hg * 4 + hl
                    mm(ps[0][:, (hg * 2 + hp) * 128:(hg * 2 + hp + 1) * 128], wuvp[:, h, :],
                       olat[:, hg, hl * 128:(hl + 1) * 128], e2 == 0, e2 == 1, ["wuvp", f"olat{hg}"], ["ps0"])
        cp("act", attT[:, :, tsl], ps[0][:, :].rearrange("p (c t) -> p c t", c=4), ["ps0"], ["attT"])

    def out_proj_block(tb, ws_, ob=6):
        tsl = slice(tb * 128, (tb + 1) * 128)
        for dh in range(2):
            b = ob + dh
            for kc in range(NKC):
                s_, k = ws_[kc]
                src = attT[:, kc, tsl] if kc < 4 else gmT[:, kc - 4, tsl]
                mm(ps[b][:, :], src, pool_sb[:, s_, dh * 512:(dh + 1) * 512], kc == 0, kc == NKC - 1,
                   [k, "attT", "gmT"], [f"ps{b}"])
        layernorm_block(tb, [ob, ob + 1], x_res[:, tb, :], f"x_res{tb}")

    store_ops = []
    make_xT()
    for mt in range(NMT):
        if mt > 0:
            load_ln(0)
        ffn(0)
        if mt == 0:
            cast_ffn_up(1)
            cast_ffn_down(1)
        layernorm_tile(lambda tb: [tb * 2, tb * 2 + 1], lambda tb: (x_res[:, tb, :], f"x_res{tb}"), make_xT_block)
        load_ln(1)
        proj_feature_major(mt)
        pg = ptm_gen(mt)
        next(pg)
        ws_ = [acquire() for _ in range(NKC)]
        indexer_a(mt, 0, bg=pg, nsteps=2 * NTB)
        for _ in pg:
            pass
        indexer_b(mt, 0)
        pending = None
        for tb in range(NTB):
            g = attention_gen(mt, tb)
            if tb + 1 < NTB:
                indexer_a(mt, tb + 1, bg=g, nsteps=2 * (mt * NTB + tb + 1), deferred=pending, is_attn_bg=True)
            elif pending is not None:
                pending()
            pending = None
            for _ in g:
                pass
            if tb + 1 < NTB:
                indexer_b(mt, tb + 1)
                pending = (lambda tb=tb, ws_=ws_: (out_proj_block(tb, ws_, ob=0), make_xT_block(tb)))
            else:
                out_proj_block(tb, ws_)
                make_xT_block(tb)
        for _ in range(NKC):
            release()
        load_ln(2)
        ffn(1)

        def after3(tb, mt=mt):
            r0 = mt * TM + tb * 128
            op = dma("pool", out_d[r0:r0 + 128, :], obuf[:, tb % 2, :], [f"obuf{tb % 2}"], [], dsem=f"ost{tb % 2}")
            store_ops.append(op)
            if mt + 1 < NMT:
                load_x_block(mt + 1, tb)
                make_xT_block(tb)

        layernorm_tile(lambda tb: [tb * 2, tb * 2 + 1], lambda tb: (obuf[:, tb % 2, :], f"obuf{tb % 2}"), after3)
    E("pool", lambda e: e.memset(junk[:, 0:1], 0.0), [], ["junk"], extra=[store_ops[-1], store_ops[-2]])

    S.finalize()
    sem_names = set()
    for e in S.ENGS:
        for op in S.ops[e]:
            if op.dsem is not None:
                sem_names.add(op.dsem)
    sems = {n: es.enter_context(nc.semaphore("d_" + n)) for n in sorted(sem_names)}
    esem = {e: es.enter_context(nc.semaphore("e_" + e)) for e in S.ENGS}

    def replay(engname, eng):
        for op in S.ops[engname]:
            for d in op.waits:
                if d.dsem is not None:
                    eng.wait_ge(sems[d.dsem], d.dval)
                else:
                    eng.wait_ge(esem[d.eng], d.val)
            ins = op.fn(eng)
            if op.dsem is not None:
                ins.then_inc(sems[op.dsem], 16)
            elif op.awaited:
                ins.then_inc(esem[engname], 1)

    with nc.Block() as block:
        @block.sync
        def _(e):
            replay("sp", e)

        @block.scalar
        def _(e):
            replay("act", e)

        @block.vector
        def _(e):
            replay("dve", e)

        @block.gpsimd
        def _(e):
            replay("pool", e)

        @block.tensor
        def _(e):
            replay("pe", e)

    es.close()
    return nc


def _consts():
    ident = np.eye(128, dtype=np.float32)
    t = np.arange(128)[:, None]
    s = np.arange(128)[None, :]
    trineg = np.where(s <= t, 0.0, NEG).astype(np.float32)
    tril = (np.arange(128)[:, None] <= np.arange(128)[None, :]).astype(np.float32)
    ramp = (-TIE_EPS * np.arange(512, dtype=np.float64)).astype(np.float32)[None, :]
    pow2 = (2.0 ** -(np.arange(32, dtype=np.float64) + 1.0)).astype(np.float32)[None, :]
    return {"c_ident": ident, "c_trineg": trineg, "c_tril": tril, "c_ramp": ramp, "c_pow2": pow2}


_W_NAMES = ["ffn1_w1", "ffn1_w3", "ffn1_w2", "ln1_g", "ln1_b", "w_in", "kv_norm_g", "w_uk", "w_uv",
            "gmlp_ln_g", "gmlp_ln_b", "gmlp_ws", "gmlp_bs", "w_out", "ln2_g", "ln2_b",
            "ffn2_w1", "ffn2_w3", "ffn2_w2", "ln3_g", "ln3_b"]


def _prep_weights(inputs):
    m = {}
    for n in _W_NAMES:
        a = np.ascontiguousarray(np.asarray(inputs[n], dtype=np.float32)[0])
        if a.ndim == 1:
            a = a[None, :]
        m[n] = a
    m.update(_consts())
    return m


def kernel(**inputs):
    x = np.asarray(inputs["x"], dtype=np.float32)
    B, L, _ = x.shape
    nc = build_nc(L)
    wm = _prep_weights(inputs)
    in_maps = []
    for b in range(B):
        d = dict(wm)
        d["x"] = np.ascontiguousarray(x[b])
        in_maps.append(d)
    res = run_bass_kernel_spmd(nc, in_maps, core_ids=list(range(B)))
    return np.stack([np.asarray(r["out"], dtype=np.float32) for r in res.results], axis=0)
```

```python
from contextlib import ExitStack
import numpy as np
import concourse.bass as bass
import concourse.mybir as mybir
from concourse.bass_utils import run_bass_kernel_spmd

F32 = mybir.dt.float32
BF16 = mybir.dt.bfloat16
AF = mybir.ActivationFunctionType
ALU = mybir.AluOpType
AX = mybir.AxisListType

D = 1024
DFF = 2816
NFC = 22
NKC = 8
TM = 512
NTB = 4
LAT = 128
TOPK = 256
ALPHA = float(2.0 ** 0.25)
EPS = 1e-5
NIT = 22
NSLOT = 10
SLOT = 1152
TIE_EPS = 1.0e-9
NEG = -1.0e30


class _Op:
    __slots__ = ("eng", "fn", "waits", "awaited", "idx", "dsem", "dval", "val")


class Sched:
    ENGS = ("pe", "act", "dve", "pool", "sp")

    def __init__(self):
        self.ops = {e: [] for e in self.ENGS}
        self.state = {}
        self.seen = {e: {} for e in self.ENGS}
        self.dma_cnt = {}

    def emit(self, eng, fn, reads=(), writes=(), dsem=None, extra=()):
        op = _Op()
        op.eng = eng
        op.fn = fn
        op.awaited = False
        op.dsem = dsem
        op.dval = None
        op.val = None
        deps = []
        for k in reads:
            st = self.state.get(k)
            if st is not None and st[0] is not None:
                deps.append(st[0])
        for k in writes:
            st = self.state.get(k)
            if st is not None:
                if st[0] is not None:
                    deps.append(st[0])
                deps.extend(st[1])
        deps.extend(extra)
        seen = self.seen[eng]
        best = {}
        for d in deps:
            if d.dsem is not None:
                key = ("d", d.dsem)
                v = d.dval
            else:
                if d.eng == "pe" and eng == "pe":
                    continue
                key = ("e", d.eng)
                v = d.idx
            if v <= seen.get(key, -1):
                continue
            if key not in best or best[key][0] < v:
                best[key] = (v, d)
        waits = []
        for key, (v, d) in best.items():
            seen[key] = v
            d.awaited = True
            waits.append(d)
        op.waits = waits
        op.idx = len(self.ops[eng])
        self.ops[eng].append(op)
        if dsem is not None:
            c = self.dma_cnt.get(dsem, 0) + 16
            self.dma_cnt[dsem] = c
            op.dval = c
        for k in reads:
            st = self.state.setdefault(k, [None, []])
            st[1].append(op)
        for k in writes:
            self.state[k] = [op, []]
        return op

    def finalize(self):
        for e in self.ENGS:
            c = 0
            for op in self.ops[e]:
                if op.dsem is None and op.awaited:
                    c += 1
                    op.val = c


def build_nc(L):
    NMT = L // TM
    NBLK = L // 128
    nc = bass.Bass("TRN2", target_bir_lowering=False)
    S = Sched()
    es = ExitStack()

    def din(name, shape, dt=F32):
        return nc.dram_tensor(name, list(shape), dt, kind="ExternalInput").ap()

    def dscr(name, shape, dt=BF16):
        return nc.dram_tensor(name, list(shape), dt, kind="Internal").ap()

    x_d = din("x", [L, D])
    out_d = nc.dram_tensor("out", [L, D], F32, kind="ExternalOutput").ap()
    w1_d = [din("ffn1_w1", [D, DFF]), din("ffn2_w1", [D, DFF])]
    w3_d = [din("ffn1_w3", [D, DFF]), din("ffn2_w3", [D, DFF])]
    w2_d = [din("ffn1_w2", [DFF, D]), din("ffn2_w2", [DFF, D])]
    lng_d = [din("ln1_g", [1, D]), din("ln2_g", [1, D]), din("ln3_g", [1, D])]
    lnb_d = [din("ln1_b", [1, D]), din("ln2_b", [1, D]), din("ln3_b", [1, D])]
    win_d = din("w_in", [D, 1988])
    kvg_d = din("kv_norm_g", [1, LAT])
    wuk_d = din("w_uk", [8, LAT, 64])
    wuv_d = din("w_uv", [8, LAT, 64])
    glg_d = din("gmlp_ln_g", [1, 512])
    glb_d = din("gmlp_ln_b", [1, 512])
    ws_d = din("gmlp_ws", [8, 128, 128])
    bs_d = din("gmlp_bs", [8, 128])
    wout_d = din("w_out", [D, D])
    ident_d = din("c_ident", [128, 128])
    trineg_d = din("c_trineg", [128, 128])
    tril_d = din("c_tril", [128, 128])
    ramp_d = din("c_ramp", [1, 512])
    pow2_d = din("c_pow2", [1, 32])

    w1s = [dscr(f"w1s{f}", [NFC, 128, NKC * 128]) for f in range(2)]
    w3s = [dscr(f"w3s{f}", [NFC, 128, NKC * 128]) for f in range(2)]
    w2s = [dscr(f"w2s{f}", [NFC, 128, D]) for f in range(2)]
    winA = dscr("winA", [7, 128, NKC * 128])
    winZ = dscr("winZ", [NKC, 128, 1024])
    winC = dscr("winC", [128, NKC * 132])
    wouts = dscr("wouts", [NKC, 128, D])

    def sb(name, shape, dt=F32):
        return es.enter_context(nc.sbuf_tensor(name, list(shape), dt))

    x_res = sb("x_res", [128, NTB, D])
    u_sb2 = sb("u_sb", [128, 2, D])
    u_sb = u_sb2[:, 0, :]
    stg = u_sb2[:, 0, :].rearrange("p (a b) -> p a b", a=8)
    obuf = sb("obuf", [128, 2, D])
    xT = sb("xT", [128, NKC, TM], BF16)
    arena = sb("arena", [128, 12288], BF16)
    pool_sb = sb("wpool", [128, NSLOT, SLOT], BF16)
    lng_sb = sb("lng", [128, D])
    lnb_sb = sb("lnb", [128, D])
    glg_sb = sb("glg", [128, 512])
    glb_sb = sb("glb", [128, 512])
    kvg_sb = sb("kvg", [128, LAT])
    cT = sb("cT", [128, L], BF16)
    c_tok = sb("c_tok", [128, NBLK, LAT], BF16)
    kiT = sb("kiT", [128, L], BF16)
    rtmp = sb("rtmp", [128, 2, 512])
    maskT = sb("maskT", [128, 2, NBLK, 128], BF16)
    qT = sb("qT", [128, 4, TM], BF16)
    qabsT = sb("qabsT", [128, 8, TM], BF16)
    qiT = sb("qiT", [128, 2, TM], BF16)
    w_tok = sb("w_tok", [128, NTB, 4])
    ug = sb("ug", [128, 512])
    vg = sb("vg", [128, 512])
    vn = sb("vn", [128, 512], BF16)
    gm_tok = sb("gm_tok", [128, 512], BF16)
    gmT = sb("gmT", [128, 4, TM], BF16)
    attT = sb("attT", [128, 4, TM], BF16)
    PT = sb("PT", [128, 3, 512], BF16)
    rden = sb("rden", [128, 512])
    sgb = sb("sgb", [128, 2, 512], BF16)
    olat = sb("olat", [128, 2, 512], BF16)
    wukT = sb("wukT", [128, 4, 128], BF16)
    wuvp = sb("wuvp", [128, 8, 128], BF16)
    wsT = sb("wsT", [128, 8, 128], BF16)
    bs_tok = sb("bs_tok", [128, 8])
    ident_f = sb("ident_f", [128, 128])
    ident_b = sb("ident_b", [128, 128], BF16)
    ones_b = sb("ones_b", [128, 128], BF16)
    trineg = sb("trineg", [128, 128])
    tril = sb("tril", [128, 128])
    ramp = sb("ramp", [128, 512])
    stats = sb("stats", [128, 2, 6])
    mv = sb("mv", [128, 2])
    sm = sb("sm", [128, 16])
    lstats = sb("lstats", [128, 2, 12])
    lmv = sb("lmv", [128, 2, 2])
    lsm = sb("lsm", [128, 2, 4])
    bis = sb("bis", [128, 8])
    p2t = sb("p2t", [128, 32])
    wtab = sb("wtab", [128, 32])
    w2tab = sb("w2tab", [128, 32])
    midt = sb("midt", [128, 32])
    cntt = sb("cntt", [128, 32])
    sgnt = sb("sgnt", [128, 32])
    c2t = sb("c2t", [128, 32])
    at = sb("at", [128, 32])
    mmt = sb("mmt", [128, 32])
    lnd = sb("lnd", [128, 512])
    junk = sb("junk", [128, 128])

    ps = [es.enter_context(nc.psum_tensor(f"ps{i}", [128, 512], F32)) for i in range(8)]
    psb = [p.bitcast(BF16) for p in ps]

    gT = arena
    scores = arena.bitcast(F32)
    maskA = arena

    def gTs(fc, lo, hi):
        return gT[:, fc * TM + lo: fc * TM + hi]

    E = S.emit

    def dma(q, out, in_, reads, writes, dsem, extra=()):
        return E(q, lambda e: e.dma_start(out=out, in_=in_), reads, writes, dsem=dsem, extra=extra)

    def mm(out, lhsT, rhs, start, stop, reads, writes):
        return E("pe", lambda e: e.matmul(out, lhsT, rhs, start=start, stop=stop), reads, writes)

    def tr(out, in_, ident, reads, writes):
        return E("pe", lambda e: e.transpose(out, in_, ident), reads, writes)

    def act(out, in_, func, reads, writes, bias=None, scale=None, accum_out=None):
        kw = {}
        if bias is not None:
            kw["bias"] = bias
        if scale is not None:
            kw["scale"] = scale
        if accum_out is not None:
            kw["accum_out"] = accum_out
        return E("act", lambda e: e.activation(out=out, in_=in_, func=func, **kw), reads, writes)

    def ts(eng, out, in0, s1, s2, op0, op1, reads, writes, accum_out=None):
        if op1 is None:
            return E(eng, lambda e: e.tensor_scalar(out, in0, s1, None, op0), reads, writes)
        if accum_out is not None:
            return E(eng, lambda e: e.tensor_scalar(out, in0, s1, s2, op0, op1, accum_out=accum_out), reads, writes)
        return E(eng, lambda e: e.tensor_scalar(out, in0, s1, s2, op0, op1), reads, writes)

    def tt(eng, out, in0, in1, op, reads, writes):
        return E(eng, lambda e: e.tensor_tensor(out, in0, in1, op), reads, writes)

    def stt(out, in0, scalar, in1, op0, op1, reads, writes):
        return E("dve", lambda e: e.scalar_tensor_tensor(out, in0, scalar, in1, op0, op1), reads, writes)

    def cp(eng, out, in_, reads, writes):
        if eng == "act":
            return E("act", lambda e: e.copy(out=out, in_=in_), reads, writes)
        return E(eng, lambda e: e.tensor_copy(out, in_), reads, writes)

    def bc(ap1, n):
        return bass.AP(ap1.tensor, 0, [[0, 128], [1, n]])

    setup_ops = []

    def setup_dma(out, in_, key):
        op = dma("pool", out, in_, [], [key], dsem="setup")
        setup_ops.append(op)
        return op

    setup_dma(ident_f[:], ident_d, "ident_f")
    setup_dma(trineg[:], trineg_d, "trineg")
    setup_dma(tril[:], tril_d, "tril")
    setup_dma(ramp[:], bc(ramp_d, 512), "ramp")
    setup_dma(p2t[:], bc(pow2_d, 32), "p2t")
    setup_dma(glg_sb[:], bc(glg_d, 512), "glg")
    setup_dma(glb_sb[:], bc(glb_d, 512), "glb")
    setup_dma(kvg_sb[:], bc(kvg_d, LAT), "kvg")
    setup_dma(stg[:, :, 0:64], wuk_d.rearrange("h c d -> c h d"), "u_sb")
    tot = S.dma_cnt["setup"]
    for op in setup_ops:
        op.dval = tot

    E("dve", lambda e: e.tensor_copy(ident_b[:], ident_f[:]), ["ident_f"], ["ident_b"])
    E("pool", lambda e: e.memset(ones_b[:], 1.0), [], ["ones_b"])
    E("pool", lambda e: e.memset(wuvp[:], 0.0), [], ["wuvp"])
    for h in range(0, 8, 2):
        b = ps[(h // 2) % 2]
        key = f"ps{(h // 2) % 2}"
        tr(b[0:64, 0:128], stg[:, h, 0:64], ident_f[:], ["u_sb", "ident_f"], [key])
        cp("dve", wukT[0:64, h // 2, :], b[0:64, 0:128], [key], ["wukT"])
    E("pool", lambda e: e.memset(junk[:], 0.0), [], ["junk"])
    for h in range(1, 8, 2):
        cp("dve", junk[:, 64:128], stg[:, h, 0:64], ["u_sb"], ["junk"])
        b = ps[2 + (h // 2) % 2]
        key = f"ps{2 + (h // 2) % 2}"
        tr(b[:, 0:128], junk[:], ident_f[:], ["junk", "ident_f"], [key])
        cp("dve", wukT[64:128, h // 2, :], b[64:128, 0:128], [key], ["wukT"])
    op_uv = dma("pool", stg[:, :, 0:64], wuv_d.rearrange("h c d -> c h d"), [], ["u_sb"], dsem="setup2")
    for h in range(8):
        off = (h % 2) * 64
        cp("dve", wuvp[:, h, off:off + 64], stg[:, h, 0:64], ["u_sb"], ["wuvp"])
    op_ws = dma("pool", stg[:], ws_d.rearrange("g t s -> t g s"), [], ["u_sb"], dsem="setup2")
    for g in range(8):
        b = ps[4 + g % 2]
        key = f"ps{4 + g % 2}"
        tr(b[:, 0:128], stg[:, g, :], ident_f[:], ["u_sb", "ident_f"], [key])
        tt("dve", wsT[:, g, :], b[:, 0:128], tril[:], ALU.mult, [key, "tril"], ["wsT"])
    op_bs = dma("pool", stg[0:8, 0, :], bs_d, [], ["u_sb"], dsem="setup2")
    tr(ps[6][:, 0:8], stg[0:8, 0, :], ident_f[0:8, 0:8], ["u_sb", "ident_f"], ["ps6"])
    cp("dve", bs_tok[:], ps[6][:, 0:8], ["ps6"], ["bs_tok"])

    cast_hist = []

    def cast(out, in_, key):
        n = len(cast_hist)
        extra = [cast_hist[n - 8]] if n >= 8 else []
        op = dma("pool", out, in_, [], [key], dsem=f"cast{n % 8}", extra=extra)
        cast_hist.append(op)

    def cast_ffn_list(f):
        lst = []
        for fc in range(NFC):
            lst.append(lambda fc=fc: cast(w1s[f][fc].rearrange("p (kc f) -> p kc f", kc=NKC),
                                          w1_d[f][:, fc * 128:(fc + 1) * 128].rearrange("(kc p) f -> p kc f", p=128),
                                          f"w1s{f}.{fc}"))
            lst.append(lambda fc=fc: cast(w3s[f][fc].rearrange("p (kc f) -> p kc f", kc=NKC),
                                          w3_d[f][:, fc * 128:(fc + 1) * 128].rearrange("(kc p) f -> p kc f", p=128),
                                          f"w3s{f}.{fc}"))
        for fc in range(NFC):
            lst.append(lambda fc=fc: cast(w2s[f][fc], w2_d[f][fc * 128:(fc + 1) * 128, :], f"w2s{f}.{fc}"))
        return lst

    def cast_ffn_up(f):
        for fn_ in cast_ffn_list(f)[:2 * NFC]:
            fn_()

    def cast_ffn_down(f):
        for fn_ in cast_ffn_list(f)[2 * NFC:]:
            fn_()

    late_casts = cast_ffn_list(1)

    def emit_late_casts(k):
        for _ in range(k):
            if late_casts:
                late_casts.pop(0)()

    for tb in range(NTB):
        r0_ = tb * 128
        dma("pool", x_res[:, tb, :], x_d[r0_:r0_ + 128, :], [], [f"x_res{tb}"], dsem=f"xin{tb}")
    dma("pool", lng_sb[:], bc(lng_d[0], D), [], ["lng"], dsem="lng")
    dma("pool", lnb_sb[:], bc(lnb_d[0], D), [], ["lnb"], dsem="lnb")
    cast_ffn_up(0)
    cast_ffn_down(0)
    colsA = [(0, 128), (128, 256), (256, 384), (384, 512), (640, 768), (768, 896)]
    for ci, (a, b_) in enumerate(colsA):
        cast(winA[ci].rearrange("p (kc f) -> p kc f", kc=NKC),
             win_d[:, a:b_].rearrange("(kc p) f -> p kc f", p=128), f"winA.{ci}")
    for half in range(2):
        cast(winA[6].rearrange("p (kc f) -> p kc f", kc=NKC)[:, :, half * 64:(half + 1) * 64],
             win_d[:, 896:960].rearrange("(kc p) f -> p kc f", p=128), f"winA.6.{half}")
    for kc in range(NKC):
        cast(winZ[kc], win_d[kc * 128:(kc + 1) * 128, 964:1988], f"winZ.{kc}")
    cast(winC.rearrange("p (kc f) -> p kc f", kc=NKC)[:, :, 0:128],
         win_d[:, 512:640].rearrange("(kc p) f -> p kc f", p=128), "winC.a")
    cast(winC.rearrange("p (kc f) -> p kc f", kc=NKC)[:, :, 128:132],
         win_d[:, 960:964].rearrange("(kc p) f -> p kc f", p=128), "winC.b")
    for kc in range(NKC):
        cast(wouts[kc], wout_d[kc * 128:(kc + 1) * 128, :], f"wouts.{kc}")

    chunks = []
    for mt in range(NMT):
        for f in range(2):
            if f == 1:
                for ci in range(7):
                    keys = [f"winA.{ci}"] if ci < 6 else ["winA.6.0", "winA.6.1"]
                    chunks.append((winA[ci], keys, 1024))
                for kc in range(NKC):
                    chunks.append((winZ[kc], [f"winZ.{kc}"], 1024))
                chunks.append((winC, ["winC.a", "winC.b"], NKC * 132))
                for kc in range(NKC):
                    chunks.append((wouts[kc], [f"wouts.{kc}"], 1024))
            for fc in range(NFC):
                chunks.append((w1s[f][fc], [f"w1s{f}.{fc}"], 1024))
                chunks.append((w3s[f][fc], [f"w3s{f}.{fc}"], 1024))
            for fc in range(NFC):
                chunks.append((w2s[f][fc], [f"w2s{f}.{fc}"], 1024))
    stream = {"next_load": 0, "next_use": 0}

    def load_chunk():
        n = stream["next_load"]
        if n >= len(chunks):
            return
        src, keys, ne = chunks[n]
        s = n % NSLOT
        dma("sp", pool_sb[:, s, 0:ne], src, keys, [f"slot{s}"], dsem=f"slot{s}")
        stream["next_load"] = n + 1

    def acquire():
        n = stream["next_use"]
        stream["next_use"] = n + 1
        s = n % NSLOT
        return s, f"slot{s}"

    def release():
        load_chunk()

    for _ in range(NSLOT):
        load_chunk()

    def load_x_block(mt, tb):
        r0 = mt * TM + tb * 128
        dma("pool", x_res[:, tb, :], x_d[r0:r0 + 128, :], [], [f"x_res{tb}"], dsem=f"xin{tb}")

    def make_xT_block(tb):
        tsl = slice(tb * 128, (tb + 1) * 128)
        for half in range(2):
            b = half
            for k4 in range(4):
                kc = half * 4 + k4
                tr(ps[b][:, k4 * 128:(k4 + 1) * 128], x_res[:, tb, kc * 128:(kc + 1) * 128], ident_f[:],
                   [f"x_res{tb}", "ident_f"], [f"ps{b}"])
            keys = [f"xT{half * 4 + k4}" for k4 in range(4)]
            dst = xT[:, half * 4:half * 4 + 4, tsl]
            src = ps[b][:, :].rearrange("p (k t) -> p k t", k=4)
            cp("act", dst, src, [f"ps{b}"], keys)

    def make_xT():
        for tb in range(NTB):
            make_xT_block(tb)

    xT_keys = [f"xT{kc}" for kc in range(NKC)]

    def ffn(f):
        for fc in range(NFC):
            s1, k1 = acquire()
            s3, k3 = acquire()
            b1 = 2 + (fc % 2) * 2
            b3 = b1 + 1
            for kc in range(NKC):
                mm(ps[b1][:, :], pool_sb[:, s1, kc * 128:(kc + 1) * 128], xT[:, kc, :], kc == 0, kc == NKC - 1,
                   [k1, f"xT{kc}"], [f"ps{b1}"])
            for kc in range(NKC):
                mm(ps[b3][:, :], pool_sb[:, s3, kc * 128:(kc + 1) * 128], xT[:, kc, :], kc == 0, kc == NKC - 1,
                   [k3, f"xT{kc}"], [f"ps{b3}"])
            release()
            release()
            sg = sgb[:, fc % 2, :]
            act(sg, ps[b1][:, :], AF.Silu, [f"ps{b1}"], [f"sg{fc % 2}"])
            stt(gTs(fc, 0, TM), ps[b3][:, :], 0.5, sg, ALU.mult, ALU.mult, [f"ps{b3}", f"sg{fc % 2}"], ["arena", "mkA", "mkB"])
        for fc in range(NFC):
            s2, k2 = acquire()
            for tb in range(NTB):
                for dh in range(2):
                    b = tb * 2 + dh
                    mm(ps[b][:, :], gTs(fc, tb * 128, (tb + 1) * 128), pool_sb[:, s2, dh * 512:(dh + 1) * 512],
                       fc == 0, fc == NFC - 1, [k2, "arena"], [f"ps{b}"])
            release()

    def load_ln(i):
        dma("pool", lng_sb[:], bc(lng_d[i], D), [], ["lng"], dsem="lng")
        dma("pool", lnb_sb[:], bc(lnb_d[i], D), [], ["lnb"], dsem="lnb")

    def ln_front(tb, banks):
        par = tb % 2
        ub = u_sb2[:, par, :]
        uk = f"u_sb{par}" if par == 1 else "u_sb"
        for dh in range(2):
            b = banks[dh]
            stt(u_sb2[:, par, dh * 512:(dh + 1) * 512], x_res[:, tb, dh * 512:(dh + 1) * 512], ALPHA, ps[b][:, :],
                ALU.mult, ALU.add, [f"x_res{tb}", f"ps{b}"], [uk])
        for dh in range(2):
            E("dve", lambda e, dh=dh: e.bn_stats(lstats[:, par, dh * 6:(dh + 1) * 6],
                                                 u_sb2[:, par, dh * 512:(dh + 1) * 512]), [uk], [f"lstats{par}"])
        E("dve", lambda e: e.bn_aggr(lmv[:, par, :], lstats[:, par, :]), [f"lstats{par}"], [f"lmv{par}"])
        ts("dve", lsm[:, par, 0:1], lmv[:, par, 1:2], EPS, None, ALU.add, None, [f"lmv{par}"], [f"lsm0{par}"])
        act(lsm[:, par, 1:2], lsm[:, par, 0:1], AF.Sqrt, [f"lsm0{par}"], [f"lsm1{par}"])
        E("dve", lambda e: e.reciprocal(lsm[:, par, 2:3], lsm[:, par, 1:2]), [f"lsm1{par}"], [f"lsm2{par}"])
        stt(lsm[:, par, 3:4], lmv[:, par, 0:1], -1.0, lsm[:, par, 2:3], ALU.mult, ALU.mult,
            [f"lmv{par}", f"lsm2{par}"], [f"lsm3{par}"])
        act(ub, ub, AF.Identity, [uk, f"lsm2{par}", f"lsm3{par}"], [uk], bias=lsm[:, par, 3:4], scale=lsm[:, par, 2:3])

    def ln_back(tb, dest, dest_key):
        par = tb % 2
        ub = u_sb2[:, par, :]
        uk = f"u_sb{par}" if par == 1 else "u_sb"
        tt("dve", dest, ub, lng_sb[:], ALU.mult, [uk, "lng"], [dest_key])
        tt("pool", dest, dest, lnb_sb[:], ALU.add, [dest_key, "lnb"], [dest_key])

    def layernorm_block(tb, banks, dest, dest_key):
        ln_front(tb, banks)
        ln_back(tb, dest, dest_key)

    def layernorm_tile(bank_fn, dest_fn, after_fn):
        def back(tb):
            d, dk = dest_fn(tb)
            ln_back(tb, d, dk)
        ln_front(0, bank_fn(0))
        ln_front(1, bank_fn(1))
        back(0)
        ln_front(2, bank_fn(2))
        after_fn(0)
        back(1)
        ln_front(3, bank_fn(3))
        after_fn(1)
        back(2)
        after_fn(2)
        back(3)
        after_fn(3)

    def proj_feature_major(mt):
        for ci in range(7):
            s, k = acquire()
            b = ci % 2
            for kc in range(NKC):
                mm(ps[b][:, :], pool_sb[:, s, kc * 128:(kc + 1) * 128], xT[:, kc, :], kc == 0, kc == NKC - 1,
                   [k, f"xT{kc}"], [f"ps{b}"])
            release()
            if ci < 4:
                cp("act", qT[:, ci, :], ps[b][:, :], [f"ps{b}"], ["qT"])
            elif ci < 6:
                cp("dve", qiT[:, ci - 4, :], ps[b][:, :], [f"ps{b}"], ["qiT"])
            else:
                cp("act", kiT[:, mt * TM:(mt + 1) * TM], ps[b][:, :], [f"ps{b}"], ["kiT"])
        for h in range(8):
            b = 2 + h % 2
            p0 = (h % 2) * 64
            mm(ps[b][:, :], wukT[p0:p0 + 64, h // 2, :], qT[p0:p0 + 64, h // 2, :], True, True,
               ["wukT", "qT"], [f"ps{b}"])
            if h % 2 == 0:
                E("act", lambda e, h=h, b=b: e.activation(out=qabsT[:, h, :], in_=ps[b][:, :], func=AF.Identity, scale=0.125),
                  [f"ps{b}"], ["qabsT"])
            else:
                ts("dve", qabsT[:, h, :], ps[b][:, :], 0.125, None, ALU.mult, None, [f"ps{b}"], ["qabsT"])

    def ptm_gen(mt):
        zs = []
        for kc in range(NKC):
            zs.append(acquire())
        sc_, kc_ = acquire()

        def zbanks(tb):
            return (4, 5) if tb % 2 == 0 else (2, 3)

        def mm_part(tb):
            tsl = slice(tb * 128, (tb + 1) * 128)
            zb = zbanks(tb)
            for half in range(2):
                b = zb[half]
                for kc in range(NKC):
                    s, k = zs[kc]
                    mm(ps[b][:, :], xT[:, kc, tsl], pool_sb[:, s, half * 512:(half + 1) * 512], kc == 0, kc == NKC - 1,
                       [k, f"xT{kc}"], [f"ps{b}"])
            for kc in range(NKC):
                mm(ps[6][:, 0:132], xT[:, kc, tsl], pool_sb[:, sc_, kc * 132:(kc + 1) * 132], kc == 0, kc == NKC - 1,
                   [kc_, f"xT{kc}"], ["ps6"])

        def elem_part(tb):
            blk = mt * NTB + tb
            zb = zbanks(tb)
            act(junk[:], ps[6][:, 0:128], AF.Square, ["ps6"], ["junk", "sm4"], accum_out=sm[:, 4:5])
            ts("dve", sm[:, 5:6], sm[:, 4:5], 1.0 / LAT, EPS, ALU.mult, ALU.add, ["sm4"], ["sm5"])
            act(sm[:, 6:7], sm[:, 5:6], AF.Sqrt, ["sm5"], ["sm6"])
            E("dve", lambda e: e.reciprocal(sm[:, 7:8], sm[:, 6:7]), ["sm6"], ["sm7"])
            stt(c_tok[:, blk, :], ps[6][:, 0:128], sm[:, 7:8], kvg_sb[:], ALU.mult, ALU.mult,
                ["ps6", "sm7", "kvg"], [f"c_tok{blk}"])
            cp("dve", w_tok[:, tb, :], ps[6][:, 128:132], ["ps6"], ["w_tok"])
            act(ug[:], ps[zb[0]][:, :], AF.Gelu_apprx_tanh, [f"ps{zb[0]}"], ["ug"])
            act(vg[:], ps[zb[1]][:, :], AF.Gelu_apprx_tanh, [f"ps{zb[1]}"], ["vg"])
            E("dve", lambda e: e.bn_stats(stats[:, 0, :], vg[:]), ["vg"], ["stats"])
            E("dve", lambda e: e.bn_aggr(mv[:], stats[:, 0, :]), ["stats"], ["mv"])
            ts("dve", sm[:, 8:9], mv[:, 1:2], EPS, None, ALU.add, None, ["mv"], ["sm8"])
            act(sm[:, 9:10], sm[:, 8:9], AF.Sqrt, ["sm8"], ["sm9"])
            E("dve", lambda e: e.reciprocal(sm[:, 10:11], sm[:, 9:10]), ["sm9"], ["sm10"])
            stt(sm[:, 11:12], mv[:, 0:1], -1.0, sm[:, 10:11], ALU.mult, ALU.mult, ["mv", "sm10"], ["sm11"])
            act(vg[:], vg[:], AF.Identity, ["vg", "sm10", "sm11"], ["vg"], bias=sm[:, 11:12], scale=sm[:, 10:11])
            tt("dve", vg[:], vg[:], glg_sb[:], ALU.mult, ["vg", "glg"], ["vg"])
            tt("pool", vn[:], vg[:], glb_sb[:], ALU.add, ["vg", "glb"], ["vn"])

        def spatial_part(tb):
            blk = mt * NTB + tb
            tsl = slice(tb * 128, (tb + 1) * 128)
            zb = zbanks(tb)
            tr(psb[7][:, 0:128], c_tok[:, blk, :], ident_b[:], [f"c_tok{blk}", "ident_b"], ["ps7"])
            cp("act", cT[:, blk * 128:(blk + 1) * 128], psb[7][:, 0:128], ["ps7"], ["cT"])
            for g in range(8):
                mm(ps[zb[0]][:, g * 64:(g + 1) * 64], wsT[:, g, :], vn[:, g * 64:(g + 1) * 64], True, True,
                   ["wsT", "vn"], [f"ps{zb[0]}"])
            tt("dve", vg[:].rearrange("p (g d) -> p g d", g=8), ps[zb[0]][:, :].rearrange("p (g d) -> p g d", g=8),
               bs_tok[:].unsqueeze(2).to_broadcast([128, 8, 64]), ALU.add, [f"ps{zb[0]}", "bs_tok"], ["vg"])
            tt("dve", gm_tok[:], vg[:], ug[:], ALU.mult, ["vg", "ug"], ["gm_tok"])
            for ch in range(4):
                tr(psb[7][:, 256 + ch * 128:256 + (ch + 1) * 128], gm_tok[:, ch * 128:(ch + 1) * 128], ident_b[:],
                   ["gm_tok", "ident_b"], ["ps7"])
            cp("act", gmT[:, :, tsl], psb[7][:, 256:768].rearrange("p (c t) -> p c t", c=4), ["ps7"], ["gmT"])

        mm_part(0)
        for tb in range(NTB):
            elem_part(tb)
            if tb + 1 < NTB:
                mm_part(tb + 1)
            yield ("elem", tb)
            spatial_part(tb)
            yield ("spatial", tb)
        for _ in range(NKC + 1):
            release()

    def indexer_a(mt, tb, bg=None, nsteps=0, deferred=None):
        i = mt * NTB + tb
        n = 128 * (i + 1)
        tsl = slice(tb * 128, (tb + 1) * 128)
        nch = (n + 511) // 512
        for c5 in range(nch):
            k0 = c5 * 512
            kn = min(512, n - k0)
            for h in range(4):
                b = h % 2
                p0 = (h % 2) * 64
                mm(ps[b][:, 0:kn], qiT[p0:p0 + 64, h // 2, tsl], kiT[p0:p0 + 64, k0:k0 + kn], True, True,
                   ["qiT", "kiT"], [f"ps{b}"])
                act(rtmp[:, b, 0:kn], ps[b][:, 0:kn], AF.Relu, [f"ps{b}"], [f"rtmp{b}"])
                in1 = ramp[:, 0:kn] if h == 0 else scores[:, k0:k0 + kn]
                rk = ["ramp"] if h == 0 else []
                stt(scores[:, k0:k0 + kn], rtmp[:, b, 0:kn], w_tok[:, tb, h:h + 1], in1, ALU.mult, ALU.add,
                    [f"rtmp{b}", "w_tok"] + rk, ["arena"])
            if c5 > 0:
                ts("dve", scores[:, k0:k0 + kn], scores[:, k0:k0 + kn], float(-TIE_EPS * k0), None, ALU.add, None,
                   [], ["arena"])
        tt("dve", scores[:, n - 128:n], scores[:, n - 128:n], trineg[:], ALU.add, ["trineg"], ["arena"])
        if deferred is not None:
            deferred()
        lo, mx, wd, av = (bis[:, 0:1], bis[:, 1:2], bis[:, 2:3], bis[:, 4:5])
        mk = maskA[:, 8192:8192 + n]
        if i < 2:
            E("dve", lambda e: e.memset(lo, -1.0e29), [], ["bis"])
        else:
            E("dve", lambda e: e.tensor_reduce(lo, scores[:, 0:n - 128], AX.X, ALU.min), ["arena"], ["bis"])
            E("dve", lambda e: e.tensor_reduce(mx, scores[:, 0:n], AX.X, ALU.max), ["arena"], ["bis"])
            tt("dve", wd, mx, lo, ALU.subtract, ["bis"], ["bis"])
            ts("dve", wd, wd, 1.0001, 1.0e-6, ALU.mult, ALU.add, ["bis"], ["bis"])
            ts("dve", wtab[:, 0:NIT + 1], p2t[:, 0:NIT + 1], wd, None, ALU.mult, None, ["bis", "p2t"], ["wtab"])
            ts("dve", w2tab[:, 0:NIT + 1], wtab[:, 0:NIT + 1], 2.0, None, ALU.mult, None, ["wtab"], ["w2tab"])
            tt("dve", midt[:, 0:1], lo, wtab[:, 0:1], ALU.add, ["bis", "wtab"], ["midt"])
            if bg is None:
                nsteps = 0
            r = -(-nsteps // NIT) if nsteps else 0
            n1 = ((224.0 + n) / 1.2 + 530.0 * r + 250.0 - 600.0) / (1.0 / 0.88 + 1.0 / 1.2)
            n1 = int(min(max(64, round(n1 / 64.0) * 64), n - 128))
            n2 = n - n1
            thr = float(TOPK) - 0.5 - 0.5 * n2
            pulled = 0
            E("dve", lambda e: e.memset(at[:, 0:1], 0.0), [], ["at"])
            E("dve", lambda e: e.tensor_copy(mmt[:, 0:1], midt[:, 0:1]), ["midt"], ["mmt"])
            for k in range(NIT):
                E("dve", lambda e, k=k: e.scalar_tensor_tensor(
                    mk[:, 0:n1], scores[:, 0:n1], mmt[:, k:k + 1], at[:, k:k + 1].to_broadcast([128, n1]),
                    ALU.subtract, ALU.is_ge, accum_out=cntt[:, k:k + 1]),
                  ["mmt", "at", "arena"], ["mkA", "cntt"])
                act(mk[:, n1:n], scores[:, n1:n], AF.Sign, ["midt", "arena"], ["mkB", "sgnt"],
                    bias=midt[:, k:k + 1], scale=-1.0, accum_out=sgnt[:, k:k + 1])
                if bg is not None:
                    for _ in range(r):
                        if pulled < nsteps - 1:
                            next(bg, None)
                            pulled += 1
                stt(mmt[:, k + 1:k + 2], at[:, k:k + 1], mmt[:, k:k + 1], wtab[:, k + 1:k + 2], ALU.add, ALU.subtract,
                    ["at", "mmt", "wtab"], ["mmt"])
                stt(c2t[:, k:k + 1], sgnt[:, k:k + 1], -0.5, cntt[:, k:k + 1], ALU.mult, ALU.add,
                    ["sgnt", "cntt"], ["c2t"])
                stt(at[:, k + 1:k + 2], c2t[:, k:k + 1], thr, w2tab[:, k + 1:k + 2], ALU.is_ge, ALU.mult,
                    ["c2t", "w2tab"], ["at"])
                act(midt[:, k + 1:k + 2], at[:, k + 1:k + 2], AF.Identity, ["mmt", "at"], ["midt"],
                    bias=mmt[:, k + 1:k + 2], scale=1.0)
            tt("dve", lo, midt[:, NIT:NIT + 1], wtab[:, NIT:NIT + 1], ALU.subtract, ["midt", "wtab"], ["bis"])
        ts("dve", mk, scores[:, 0:n], lo, None, ALU.is_ge, None, ["bis", "arena"], ["mkA", "mkB"])

    def indexer_b(mt, tb):
        i = mt * NTB + tb
        par = tb % 2
        nb = i + 1
        for gi, j0 in enumerate(range(0, nb, 4)):
            jn = min(4, nb - j0)
            bb = gi % 2
            for jj in range(jn):
                j = j0 + jj
                tr(psb[bb][:, jj * 128:(jj + 1) * 128], maskA[:, 8192 + j * 128:8192 + (j + 1) * 128], ident_b[:],
                   ["mkA", "mkB", "ident_b"], [f"ps{bb}"])
            act(maskT[:, par, j0:j0 + jn, :], psb[bb][:, 0:jn * 128].rearrange("p (j t) -> p j t", j=jn),
                AF.Identity, [f"ps{bb}"], [f"maskT{par}"], bias=-30000.0, scale=30000.0)

    def attention_gen(mt, tb):
        i = mt * NTB + tb
        par = tb % 2
        tsl = slice(tb * 128, (tb + 1) * 128)
        steps = [(hg, j) for hg in range(2) for j in range(i + 1)]

        def emit_S(idx):
            hg, j = steps[idx]
            bs_ = 6 + (idx % 2)
            mm(ps[bs_][:, :].rearrange("p (h t) -> p h t", h=4), cT[:, j * 128:(j + 1) * 128],
               qabsT[:, hg * 4:(hg + 1) * 4, tsl], True, False, ["cT", "qabsT"], [f"ps{bs_}"])
            mm(ps[bs_][:, :].rearrange("p (h t) -> p h t", h=4), ident_b[:],
               maskT[:, par, j, :].unsqueeze(1).to_broadcast([128, 4, 128]), False, True,
               ["ident_b", f"maskT{par}"], [f"ps{bs_}"])

        emit_S(0)
        if len(steps) > 1:
            emit_S(1)
        for idx, (hg, j) in enumerate(steps):
            bo, bd = (4, 5) if hg == 0 else (2, 3)
            bs_ = 6 + (idx % 2)
            pt = idx % 3
            act(PT[:, pt, :], ps[bs_][:, :], AF.Exp, [f"ps{bs_}"], [f"PT{pt}"])
            if idx + 2 < len(steps):
                emit_S(idx + 2)
            mm(ps[bo][:, :], c_tok[:, j, :], PT[:, pt, :], j == 0, j == i, [f"c_tok{j}", f"PT{pt}"], [f"ps{bo}"])
            mm(ps[bd][:, :], ones_b[:], PT[:, pt, :], j == 0, j == i, ["ones_b", f"PT{pt}"], [f"ps{bd}"])
            yield idx
        for hg in range(2):
            bo, bd = (4, 5) if hg == 0 else (2, 3)
            act(lnd[:], ps[bd][:, :], AF.Ln, [f"ps{bd}"], ["lnd"])
            act(rden[:], lnd[:], AF.Exp, ["lnd"], ["rden"], scale=-1.0)
            tt("dve", olat[:, hg, :], ps[bo][:, :], rden[:], ALU.mult, [f"ps{bo}", "rden"], [f"olat{hg}"])
            for hp in range(2):
                for e2 in range(2):
                    hl = hp * 2 + e2
                    h = hg * 4 + hl
                    mm(ps[0][:, (hg * 2 + hp) * 128:(hg * 2 + hp + 1) * 128], wuvp[:, h, :],
                       olat[:, hg, hl * 128:(hl + 1) * 128], e2 == 0, e2 == 1, ["wuvp", f"olat{hg}"], ["ps0"])
        cp("act", attT[:, :, tsl], ps[0][:, :].rearrange("p (c t) -> p c t", c=4), ["ps0"], ["attT"])

    def out_proj_block(tb, ws_):
        tsl = slice(tb * 128, (tb + 1) * 128)
        for dh in range(2):
            b = 6 + dh
            for kc in range(NKC):
                s_, k = ws_[kc]
                src = attT[:, kc, tsl] if kc < 4 else gmT[:, kc - 4, tsl]
                mm(ps[b][:, :], src, pool_sb[:, s_, dh * 512:(dh + 1) * 512], kc == 0, kc == NKC - 1,
                   [k, "attT", "gmT"], [f"ps{b}"])
        layernorm_block(tb, [6, 7], x_res[:, tb, :], f"x_res{tb}")

    store_ops = []
    make_xT()
    for mt in range(NMT):
        if mt > 0:
            load_ln(0)
        ffn(0)
        layernorm_tile(lambda tb: [tb * 2, tb * 2 + 1], lambda tb: (x_res[:, tb, :], f"x_res{tb}"), make_xT_block)
        if mt == 0:
            emit_late_casts(12)
        load_ln(1)
        proj_feature_major(mt)
        pg = ptm_gen(mt)
        next(pg)
        ws_ = [acquire() for _ in range(NKC)]
        indexer_a(mt, 0, bg=pg, nsteps=2 * NTB)
        for _ in pg:
            pass
        indexer_b(mt, 0)
        if mt == 0:
            emit_late_casts(12)
        pending = None
        for tb in range(NTB):
            if mt == 0:
                emit_late_casts(11 if tb < NTB - 1 else 99)
            g = attention_gen(mt, tb)
            if tb + 1 < NTB:
                indexer_a(mt, tb + 1, bg=g, nsteps=2 * (mt * NTB + tb + 1), deferred=pending)
            elif pending is not None:
                pending()
            pending = None
            for _ in g:
                pass
            if tb + 1 < NTB:
                indexer_b(mt, tb + 1)
                pending = (lambda tb=tb, ws_=ws_: (out_proj_block(tb, ws_), make_xT_block(tb)))
            else:
                out_proj_block(tb, ws_)
                make_xT_block(tb)
        for _ in range(NKC):
            release()
        load_ln(2)
        ffn(1)

        def after3(tb, mt=mt):
            r0 = mt * TM + tb * 128
            op = dma("pool", out_d[r0:r0 + 128, :], obuf[:, tb % 2, :], [f"obuf{tb % 2}"], [], dsem=f"ost{tb % 2}")
            store_ops.append(op)
            if mt + 1 < NMT:
                load_x_block(mt + 1, tb)
                make_xT_block(tb)

        layernorm_tile(lambda tb: [tb * 2, tb * 2 + 1], lambda tb: (obuf[:, tb % 2, :], f"obuf{tb % 2}"), after3)
    E("pool", lambda e: e.memset(junk[:, 0:1], 0.0), [], ["junk"], extra=[store_ops[-1], store_ops[-2]])

    S.finalize()
    sem_names = set()
    for e in S.ENGS:
        for op in S.ops[e]:
            if op.dsem is not None:
                sem_names.add(op.dsem)
    sems = {n: es.enter_context(nc.semaphore("d_" + n)) for n in sorted(sem_names)}
    esem = {e: es.enter_context(nc.semaphore("e_" + e)) for e in S.ENGS}

    def replay(engname, eng):
        for op in S.ops[engname]:
            for d in op.waits:
                if d.dsem is not None:
                    eng.wait_ge(sems[d.dsem], d.dval)
                else:
                    eng.wait_ge(esem[d.eng], d.val)
            ins = op.fn(eng)
            if op.dsem is not None:
                ins.then_inc(sems[op.dsem], 16)
            elif op.awaited:
                ins.then_inc(esem[engname], 1)

    with nc.Block() as block:
        @block.sync
        def _(e):
            replay("sp", e)

        @block.scalar
        def _(e):
            replay("act", e)

        @block.vector
        def _(e):
            replay("dve", e)

        @block.gpsimd
        def _(e):
            replay("pool", e)

        @block.tensor
        def _(e):
            replay("pe", e)

    es.close()
    return nc


def _consts():
    ident = np.eye(128, dtype=np.float32)
    t = np.arange(128)[:, None]
    s = np.arange(128)[None, :]
    trineg = np.where(s <= t, 0.0, NEG).astype(np.float32)
    tril = (np.arange(128)[:, None] <= np.arange(128)[None, :]).astype(np.float32)
    ramp = (-TIE_EPS * np.arange(512, dtype=np.float64)).astype(np.float32)[None, :]
    pow2 = (2.0 ** -(np.arange(32, dtype=np.float64) + 1.0)).astype(np.float32)[None, :]
    return {"c_ident": ident, "c_trineg": trineg, "c_tril": tril, "c_ramp": ramp, "c_pow2": pow2}


_W_NAMES = ["ffn1_w1", "ffn1_w3", "ffn1_w2", "ln1_g", "ln1_b", "w_in", "kv_norm_g", "w_uk", "w_uv",
            "gmlp_ln_g", "gmlp_ln_b", "gmlp_ws", "gmlp_bs", "w_out", "ln2_g", "ln2_b",
            "ffn2_w1", "ffn2_w3", "ffn2_w2", "ln3_g", "ln3_b"]


def _prep_weights(inputs):
    m = {}
    for n in _W_NAMES:
        a = np.ascontiguousarray(np.asarray(inputs[n], dtype=np.float32)[0])
        if a.ndim == 1:
            a = a[None, :]
        m[n] = a
    m.update(_consts())
    return m


def kernel(**inputs):
    x = np.asarray(inputs["x"], dtype=np.float32)
    B, L, _ = x.shape
    nc = build_nc(L)
    wm = _prep_weights(inputs)
    in_maps = []
    for b in range(B):
        d = dict(wm)
        d["x"] = np.ascontiguousarray(x[b])
        in_maps.append(d)
    res = run_bass_kernel_spmd(nc, in_maps, core_ids=list(range(B)))
    return np.stack([np.asarray(r["out"], dtype=np.float32) for r in res.results], axis=0)
```

```python
from contextlib import ExitStack
import numpy as np
import concourse.bass as bass
import concourse.mybir as mybir
from concourse.bass_utils import run_bass_kernel_spmd

F32 = mybir.dt.float32
BF16 = mybir.dt.bfloat16
AF = mybir.ActivationFunctionType
ALU = mybir.AluOpType
AX = mybir.AxisListType

D = 1024
DFF = 2816
NFC = 22
NKC = 8
TM = 512
NTB = 4
LAT = 128
TOPK = 256
ALPHA = float(2.0 ** 0.25)
EPS = 1e-5
NIT = 20
NSLOT = 10
SLOT = 1152
TIE_EPS = 1.0e-9
NEG = -1.0e30


class _Op:
    __slots__ = ("eng", "fn", "waits", "awaited", "idx", "dsem", "dval", "val")


class Sched:
    ENGS = ("pe", "act", "dve", "pool", "sp")

    def __init__(self):
        self.ops = {e: [] for e in self.ENGS}
        self.state = {}
        self.seen = {e: {} for e in self.ENGS}
        self.dma_cnt = {}

    def emit(self, eng, fn, reads=(), writes=(), dsem=None, extra=()):
        op = _Op()
        op.eng = eng
        op.fn = fn
        op.awaited = False
        op.dsem = dsem
        op.dval = None
        op.val = None
        deps = []
        for k in reads:
            st = self.state.get(k)
            if st is not None and st[0] is not None:
                deps.append(st[0])
        for k in writes:
            st = self.state.get(k)
            if st is not None:
                if st[0] is not None:
                    deps.append(st[0])
                deps.extend(st[1])
        deps.extend(extra)
        seen = self.seen[eng]
        best = {}
        for d in deps:
            if d.dsem is not None:
                key = ("d", d.dsem)
                v = d.dval
            else:
                if d.eng == "pe" and eng == "pe":
                    continue
                key = ("e", d.eng)
                v = d.idx
            if v <= seen.get(key, -1):
                continue
            if key not in best or best[key][0] < v:
                best[key] = (v, d)
        waits = []
        for key, (v, d) in best.items():
            seen[key] = v
            d.awaited = True
            waits.append(d)
        op.waits = waits
        op.idx = len(self.ops[eng])
        self.ops[eng].append(op)
        if dsem is not None:
            c = self.dma_cnt.get(dsem, 0) + 16
            self.dma_cnt[dsem] = c
            op.dval = c
        for k in reads:
            st = self.state.setdefault(k, [None, []])
            st[1].append(op)
        for k in writes:
            self.state[k] = [op, []]
        return op

    def finalize(self):
        for e in self.ENGS:
            c = 0
            for op in self.ops[e]:
                if op.dsem is None and op.awaited:
                    c += 1
                    op.val = c


def build_nc(L):
    NMT = L // TM
    NBLK = L // 128
    nc = bass.Bass("TRN2", target_bir_lowering=False)
    S = Sched()
    es = ExitStack()

    def din(name, shape, dt=F32):
        return nc.dram_tensor(name, list(shape), dt, kind="ExternalInput").ap()

    def dscr(name, shape, dt=BF16):
        return nc.dram_tensor(name, list(shape), dt, kind="Internal").ap()

    x_d = din("x", [L, D])
    out_d = nc.dram_tensor("out", [L, D], F32, kind="ExternalOutput").ap()
    w1_d = [din("ffn1_w1", [D, DFF]), din("ffn2_w1", [D, DFF])]
    w3_d = [din("ffn1_w3", [D, DFF]), din("ffn2_w3", [D, DFF])]
    w2_d = [din("ffn1_w2", [DFF, D]), din("ffn2_w2", [DFF, D])]
    lng_d = [din("ln1_g", [1, D]), din("ln2_g", [1, D]), din("ln3_g", [1, D])]
    lnb_d = [din("ln1_b", [1, D]), din("ln2_b", [1, D]), din("ln3_b", [1, D])]
    win_d = din("w_in", [D, 1988])
    kvg_d = din("kv_norm_g", [1, LAT])
    wuk_d = din("w_uk", [8, LAT, 64])
    wuv_d = din("w_uv", [8, LAT, 64])
    glg_d = din("gmlp_ln_g", [1, 512])
    glb_d = din("gmlp_ln_b", [1, 512])
    ws_d = din("gmlp_ws", [8, 128, 128])
    bs_d = din("gmlp_bs", [8, 128])
    wout_d = din("w_out", [D, D])
    ident_d = din("c_ident", [128, 128])
    trineg_d = din("c_trineg", [128, 128])
    tril_d = din("c_tril", [128, 128])
    ramp_d = din("c_ramp", [1, 512])
    pow2_d = din("c_pow2", [1, 32])

    w1s = [dscr(f"w1s{f}", [NFC, 128, NKC * 128]) for f in range(2)]
    w3s = [dscr(f"w3s{f}", [NFC, 128, NKC * 128]) for f in range(2)]
    w2s = [dscr(f"w2s{f}", [NFC, 128, D]) for f in range(2)]
    winA = dscr("winA", [7, 128, NKC * 128])
    winZ = dscr("winZ", [NKC, 128, 1024])
    winC = dscr("winC", [128, NKC * 132])
    wouts = dscr("wouts", [NKC, 128, D])

    def sb(name, shape, dt=F32):
        return es.enter_context(nc.sbuf_tensor(name, list(shape), dt))

    x_res = sb("x_res", [128, NTB, D])
    u_sb2 = sb("u_sb", [128, 2, D])
    u_sb = u_sb2[:, 0, :]
    stg = u_sb2[:, 0, :].rearrange("p (a b) -> p a b", a=8)
    obuf = sb("obuf", [128, 2, D])
    xT = sb("xT", [128, NKC, TM], BF16)
    arena = sb("arena", [128, 12288], BF16)
    pool_sb = sb("wpool", [128, NSLOT, SLOT], BF16)
    lng_sb = sb("lng", [128, D])
    lnb_sb = sb("lnb", [128, D])
    glg_sb = sb("glg", [128, 512])
    glb_sb = sb("glb", [128, 512])
    kvg_sb = sb("kvg", [128, LAT])
    cT = sb("cT", [128, L], BF16)
    c_tok = sb("c_tok", [128, NBLK, LAT], BF16)
    kiT = sb("kiT", [128, L], BF16)
    rtmp = sb("rtmp", [128, 2, 512])
    maskT = sb("maskT", [128, 2, NBLK, 128], BF16)
    qT = sb("qT", [128, 4, TM], BF16)
    qabsT = sb("qabsT", [128, 8, TM], BF16)
    qiT = sb("qiT", [128, 2, TM], BF16)
    w_tok = sb("w_tok", [128, NTB, 4])
    ug = sb("ug", [128, 512])
    vg = sb("vg", [128, 512])
    vn = sb("vn", [128, 512], BF16)
    gm_tok = sb("gm_tok", [128, 512], BF16)
    gmT = sb("gmT", [128, 4, TM], BF16)
    attT = sb("attT", [128, 4, TM], BF16)
    PT = sb("PT", [128, 3, 512], BF16)
    rden = sb("rden", [128, 512])
    sgb = sb("sgb", [128, 2, 512], BF16)
    olat = sb("olat", [128, 2, 512], BF16)
    wukT = sb("wukT", [128, 4, 128], BF16)
    wuvp = sb("wuvp", [128, 8, 128], BF16)
    wsT = sb("wsT", [128, 8, 128], BF16)
    bs_tok = sb("bs_tok", [128, 8])
    ident_f = sb("ident_f", [128, 128])
    ident_b = sb("ident_b", [128, 128], BF16)
    ones_b = sb("ones_b", [128, 128], BF16)
    trineg = sb("trineg", [128, 128])
    tril = sb("tril", [128, 128])
    ramp = sb("ramp", [128, 512])
    stats = sb("stats", [128, 2, 6])
    mv = sb("mv", [128, 2])
    sm = sb("sm", [128, 16])
    lstats = sb("lstats", [128, 2, 12])
    lmv = sb("lmv", [128, 2, 2])
    lsm = sb("lsm", [128, 2, 4])
    bis = sb("bis", [128, 8])
    p2t = sb("p2t", [128, 32])
    wtab = sb("wtab", [128, 32])
    w2tab = sb("w2tab", [128, 32])
    midt = sb("midt", [128, 32])
    cntt = sb("cntt", [128, 32])
    sgnt = sb("sgnt", [128, 32])
    c2t = sb("c2t", [128, 32])
    at = sb("at", [128, 32])
    mmt = sb("mmt", [128, 32])
    lnd = sb("lnd", [128, 512])
    junk = sb("junk", [128, 128])

    ps = [es.enter_context(nc.psum_tensor(f"ps{i}", [128, 512], F32)) for i in range(8)]
    psb = [p.bitcast(BF16) for p in ps]

    gT = arena
    scores = arena.bitcast(F32)
    maskA = arena

    def gTs(fc, lo, hi):
        return gT[:, fc * TM + lo: fc * TM + hi]

    E = S.emit

    def dma(q, out, in_, reads, writes, dsem, extra=()):
        return E(q, lambda e: e.dma_start(out=out, in_=in_), reads, writes, dsem=dsem, extra=extra)

    def mm(out, lhsT, rhs, start, stop, reads, writes):
        return E("pe", lambda e: e.matmul(out, lhsT, rhs, start=start, stop=stop), reads, writes)

    def tr(out, in_, ident, reads, writes):
        return E("pe", lambda e: e.transpose(out, in_, ident), reads, writes)

    def act(out, in_, func, reads, writes, bias=None, scale=None, accum_out=None):
        kw = {}
        if bias is not None:
            kw["bias"] = bias
        if scale is not None:
            kw["scale"] = scale
        if accum_out is not None:
            kw["accum_out"] = accum_out
        return E("act", lambda e: e.activation(out=out, in_=in_, func=func, **kw), reads, writes)

    def ts(eng, out, in0, s1, s2, op0, op1, reads, writes, accum_out=None):
        if op1 is None:
            return E(eng, lambda e: e.tensor_scalar(out, in0, s1, None, op0), reads, writes)
        if accum_out is not None:
            return E(eng, lambda e: e.tensor_scalar(out, in0, s1, s2, op0, op1, accum_out=accum_out), reads, writes)
        return E(eng, lambda e: e.tensor_scalar(out, in0, s1, s2, op0, op1), reads, writes)

    def tt(eng, out, in0, in1, op, reads, writes):
        return E(eng, lambda e: e.tensor_tensor(out, in0, in1, op), reads, writes)

    def stt(out, in0, scalar, in1, op0, op1, reads, writes):
        return E("dve", lambda e: e.scalar_tensor_tensor(out, in0, scalar, in1, op0, op1), reads, writes)

    def cp(eng, out, in_, reads, writes):
        if eng == "act":
            return E("act", lambda e: e.copy(out=out, in_=in_), reads, writes)
        return E(eng, lambda e: e.tensor_copy(out, in_), reads, writes)

    def bc(ap1, n):
        return bass.AP(ap1.tensor, 0, [[0, 128], [1, n]])

    setup_ops = []

    def setup_dma(out, in_, key):
        op = dma("pool", out, in_, [], [key], dsem="setup")
        setup_ops.append(op)
        return op

    setup_dma(ident_f[:], ident_d, "ident_f")
    setup_dma(trineg[:], trineg_d, "trineg")
    setup_dma(tril[:], tril_d, "tril")
    setup_dma(ramp[:], bc(ramp_d, 512), "ramp")
    setup_dma(p2t[:], bc(pow2_d, 32), "p2t")
    setup_dma(glg_sb[:], bc(glg_d, 512), "glg")
    setup_dma(glb_sb[:], bc(glb_d, 512), "glb")
    setup_dma(kvg_sb[:], bc(kvg_d, LAT), "kvg")
    setup_dma(stg[:, :, 0:64], wuk_d.rearrange("h c d -> c h d"), "u_sb")
    tot = S.dma_cnt["setup"]
    for op in setup_ops:
        op.dval = tot

    E("dve", lambda e: e.tensor_copy(ident_b[:], ident_f[:]), ["ident_f"], ["ident_b"])
    E("pool", lambda e: e.memset(ones_b[:], 1.0), [], ["ones_b"])
    E("pool", lambda e: e.memset(wuvp[:], 0.0), [], ["wuvp"])
    for h in range(0, 8, 2):
        b = ps[(h // 2) % 2]
        key = f"ps{(h // 2) % 2}"
        tr(b[0:64, 0:128], stg[:, h, 0:64], ident_f[:], ["u_sb", "ident_f"], [key])
        cp("dve", wukT[0:64, h // 2, :], b[0:64, 0:128], [key], ["wukT"])
    E("pool", lambda e: e.memset(junk[:], 0.0), [], ["junk"])
    for h in range(1, 8, 2):
        cp("dve", junk[:, 64:128], stg[:, h, 0:64], ["u_sb"], ["junk"])
        b = ps[2 + (h // 2) % 2]
        key = f"ps{2 + (h // 2) % 2}"
        tr(b[:, 0:128], junk[:], ident_f[:], ["junk", "ident_f"], [key])
        cp("dve", wukT[64:128, h // 2, :], b[64:128, 0:128], [key], ["wukT"])
    op_uv = dma("pool", stg[:, :, 0:64], wuv_d.rearrange("h c d -> c h d"), [], ["u_sb"], dsem="setup2")
    for h in range(8):
        off = (h % 2) * 64
        cp("dve", wuvp[:, h, off:off + 64], stg[:, h, 0:64], ["u_sb"], ["wuvp"])
    op_ws = dma("pool", stg[:], ws_d.rearrange("g t s -> t g s"), [], ["u_sb"], dsem="setup2")
    for g in range(8):
        b = ps[4 + g % 2]
        key = f"ps{4 + g % 2}"
        tr(b[:, 0:128], stg[:, g, :], ident_f[:], ["u_sb", "ident_f"], [key])
        tt("dve", wsT[:, g, :], b[:, 0:128], tril[:], ALU.mult, [key, "tril"], ["wsT"])
    op_bs = dma("pool", stg[0:8, 0, :], bs_d, [], ["u_sb"], dsem="setup2")
    tr(ps[6][:, 0:8], stg[0:8, 0, :], ident_f[0:8, 0:8], ["u_sb", "ident_f"], ["ps6"])
    cp("dve", bs_tok[:], ps[6][:, 0:8], ["ps6"], ["bs_tok"])

    cast_hist = []

    def cast(out, in_, key):
        n = len(cast_hist)
        extra = [cast_hist[n - 8]] if n >= 8 else []
        op = dma("pool", out, in_, [], [key], dsem=f"cast{n % 8}", extra=extra)
        cast_hist.append(op)

    def cast_ffn_list(f):
        lst = []
        for fc in range(NFC):
            lst.append(lambda fc=fc: cast(w1s[f][fc].rearrange("p (kc f) -> p kc f", kc=NKC),
                                          w1_d[f][:, fc * 128:(fc + 1) * 128].rearrange("(kc p) f -> p kc f", p=128),
                                          f"w1s{f}.{fc}"))
            lst.append(lambda fc=fc: cast(w3s[f][fc].rearrange("p (kc f) -> p kc f", kc=NKC),
                                          w3_d[f][:, fc * 128:(fc + 1) * 128].rearrange("(kc p) f -> p kc f", p=128),
                                          f"w3s{f}.{fc}"))
        for fc in range(NFC):
            lst.append(lambda fc=fc: cast(w2s[f][fc], w2_d[f][fc * 128:(fc + 1) * 128, :], f"w2s{f}.{fc}"))
        return lst

    def cast_ffn_up(f):
        for fn_ in cast_ffn_list(f)[:2 * NFC]:
            fn_()

    def cast_ffn_down(f):
        for fn_ in cast_ffn_list(f)[2 * NFC:]:
            fn_()

    late_casts = cast_ffn_list(1)

    def emit_late_casts(k):
        for _ in range(k):
            if late_casts:
                late_casts.pop(0)()

    for tb in range(NTB):
        r0_ = tb * 128
        dma("pool", x_res[:, tb, :], x_d[r0_:r0_ + 128, :], [], [f"x_res{tb}"], dsem=f"xin{tb}")
    dma("pool", lng_sb[:], bc(lng_d[0], D), [], ["lng"], dsem="lng")
    dma("pool", lnb_sb[:], bc(lnb_d[0], D), [], ["lnb"], dsem="lnb")
    cast_ffn_up(0)
    cast_ffn_down(0)
    colsA = [(0, 128), (128, 256), (256, 384), (384, 512), (640, 768), (768, 896)]
    for ci, (a, b_) in enumerate(colsA):
        cast(winA[ci].rearrange("p (kc f) -> p kc f", kc=NKC),
             win_d[:, a:b_].rearrange("(kc p) f -> p kc f", p=128), f"winA.{ci}")
    for half in range(2):
        cast(winA[6].rearrange("p (kc f) -> p kc f", kc=NKC)[:, :, half * 64:(half + 1) * 64],
             win_d[:, 896:960].rearrange("(kc p) f -> p kc f", p=128), f"winA.6.{half}")
    for kc in range(NKC):
        cast(winZ[kc], win_d[kc * 128:(kc + 1) * 128, 964:1988], f"winZ.{kc}")
    cast(winC.rearrange("p (kc f) -> p kc f", kc=NKC)[:, :, 0:128],
         win_d[:, 512:640].rearrange("(kc p) f -> p kc f", p=128), "winC.a")
    cast(winC.rearrange("p (kc f) -> p kc f", kc=NKC)[:, :, 128:132],
         win_d[:, 960:964].rearrange("(kc p) f -> p kc f", p=128), "winC.b")
    for kc in range(NKC):
        cast(wouts[kc], wout_d[kc * 128:(kc + 1) * 128, :], f"wouts.{kc}")

    chunks = []
    for mt in range(NMT):
        for f in range(2):
            if f == 1:
                for ci in range(7):
                    keys = [f"winA.{ci}"] if ci < 6 else ["winA.6.0", "winA.6.1"]
                    chunks.append((winA[ci], keys, 1024))
                for kc in range(NKC):
                    chunks.append((winZ[kc], [f"winZ.{kc}"], 1024))
                chunks.append((winC, ["winC.a", "winC.b"], NKC * 132))
                for kc in range(NKC):
                    chunks.append((wouts[kc], [f"wouts.{kc}"], 1024))
            for fc in range(NFC):
                chunks.append((w1s[f][fc], [f"w1s{f}.{fc}"], 1024))
                chunks.append((w3s[f][fc], [f"w3s{f}.{fc}"], 1024))
            for fc in range(NFC):
                chunks.append((w2s[f][fc], [f"w2s{f}.{fc}"], 1024))
    stream = {"next_load": 0, "next_use": 0}

    def load_chunk():
        n = stream["next_load"]
        if n >= len(chunks):
            return
        src, keys, ne = chunks[n]
        s = n % NSLOT
        dma("sp", pool_sb[:, s, 0:ne], src, keys, [f"slot{s}"], dsem=f"slot{s}")
        stream["next_load"] = n + 1

    def acquire():
        n = stream["next_use"]
        stream["next_use"] = n + 1
        s = n % NSLOT
        return s, f"slot{s}"

    def release():
        load_chunk()

    for _ in range(NSLOT):
        load_chunk()

    def load_x_block(mt, tb):
        r0 = mt * TM + tb * 128
        dma("pool", x_res[:, tb, :], x_d[r0:r0 + 128, :], [], [f"x_res{tb}"], dsem=f"xin{tb}")

    def make_xT_block(tb):
        tsl = slice(tb * 128, (tb + 1) * 128)
        for half in range(2):
            b = half
            for k4 in range(4):
                kc = half * 4 + k4
                tr(ps[b][:, k4 * 128:(k4 + 1) * 128], x_res[:, tb, kc * 128:(kc + 1) * 128], ident_f[:],
                   [f"x_res{tb}", "ident_f"], [f"ps{b}"])
            keys = [f"xT{half * 4 + k4}" for k4 in range(4)]
            dst = xT[:, half * 4:half * 4 + 4, tsl]
            src = ps[b][:, :].rearrange("p (k t) -> p k t", k=4)
            cp("act", dst, src, [f"ps{b}"], keys)

    def make_xT():
        for tb in range(NTB):
            make_xT_block(tb)

    xT_keys = [f"xT{kc}" for kc in range(NKC)]

    def ffn(f):
        for fc in range(NFC):
            s1, k1 = acquire()
            s3, k3 = acquire()
            b1 = 2 + (fc % 2) * 2
            b3 = b1 + 1
            for kc in range(NKC):
                mm(ps[b1][:, :], pool_sb[:, s1, kc * 128:(kc + 1) * 128], xT[:, kc, :], kc == 0, kc == NKC - 1,
                   [k1, f"xT{kc}"], [f"ps{b1}"])
            for kc in range(NKC):
                mm(ps[b3][:, :], pool_sb[:, s3, kc * 128:(kc + 1) * 128], xT[:, kc, :], kc == 0, kc == NKC - 1,
                   [k3, f"xT{kc}"], [f"ps{b3}"])
            release()
            release()
            sg = sgb[:, fc % 2, :]
            act(sg, ps[b1][:, :], AF.Silu, [f"ps{b1}"], [f"sg{fc % 2}"])
            stt(gTs(fc, 0, TM), ps[b3][:, :], 0.5, sg, ALU.mult, ALU.mult, [f"ps{b3}", f"sg{fc % 2}"], ["arena", "mkA", "mkB"])
        for fc in range(NFC):
            s2, k2 = acquire()
            for tb in range(NTB):
                for dh in range(2):
                    b = tb * 2 + dh
                    mm(ps[b][:, :], gTs(fc, tb * 128, (tb + 1) * 128), pool_sb[:, s2, dh * 512:(dh + 1) * 512],
                       fc == 0, fc == NFC - 1, [k2, "arena"], [f"ps{b}"])
            release()

    def load_ln(i):
        dma("pool", lng_sb[:], bc(lng_d[i], D), [], ["lng"], dsem="lng")
        dma("pool", lnb_sb[:], bc(lnb_d[i], D), [], ["lnb"], dsem="lnb")

    def ln_front(tb, banks):
        par = tb % 2
        ub = u_sb2[:, par, :]
        uk = f"u_sb{par}" if par == 1 else "u_sb"
        for dh in range(2):
            b = banks[dh]
            stt(u_sb2[:, par, dh * 512:(dh + 1) * 512], x_res[:, tb, dh * 512:(dh + 1) * 512], ALPHA, ps[b][:, :],
                ALU.mult, ALU.add, [f"x_res{tb}", f"ps{b}"], [uk])
        for dh in range(2):
            E("dve", lambda e, dh=dh: e.bn_stats(lstats[:, par, dh * 6:(dh + 1) * 6],
                                                 u_sb2[:, par, dh * 512:(dh + 1) * 512]), [uk], [f"lstats{par}"])
        E("dve", lambda e: e.bn_aggr(lmv[:, par, :], lstats[:, par, :]), [f"lstats{par}"], [f"lmv{par}"])
        ts("dve", lsm[:, par, 0:1], lmv[:, par, 1:2], EPS, None, ALU.add, None, [f"lmv{par}"], [f"lsm0{par}"])
        act(lsm[:, par, 1:2], lsm[:, par, 0:1], AF.Sqrt, [f"lsm0{par}"], [f"lsm1{par}"])
        E("dve", lambda e: e.reciprocal(lsm[:, par, 2:3], lsm[:, par, 1:2]), [f"lsm1{par}"], [f"lsm2{par}"])
        stt(lsm[:, par, 3:4], lmv[:, par, 0:1], -1.0, lsm[:, par, 2:3], ALU.mult, ALU.mult,
            [f"lmv{par}", f"lsm2{par}"], [f"lsm3{par}"])
        act(ub, ub, AF.Identity, [uk, f"lsm2{par}", f"lsm3{par}"], [uk], bias=lsm[:, par, 3:4], scale=lsm[:, par, 2:3])

    def ln_back(tb, dest, dest_key):
        par = tb % 2
        ub = u_sb2[:, par, :]
        uk = f"u_sb{par}" if par == 1 else "u_sb"
        tt("dve", dest, ub, lng_sb[:], ALU.mult, [uk, "lng"], [dest_key])
        tt("pool", dest, dest, lnb_sb[:], ALU.add, [dest_key, "lnb"], [dest_key])

    def layernorm_block(tb, banks, dest, dest_key):
        ln_front(tb, banks)
        ln_back(tb, dest, dest_key)

    def layernorm_tile(bank_fn, dest_fn, after_fn):
        def back(tb):
            d, dk = dest_fn(tb)
            ln_back(tb, d, dk)
        ln_front(0, bank_fn(0))
        ln_front(1, bank_fn(1))
        back(0)
        ln_front(2, bank_fn(2))
        after_fn(0)
        back(1)
        ln_front(3, bank_fn(3))
        after_fn(1)
        back(2)
        after_fn(2)
        back(3)
        after_fn(3)

    def proj_feature_major(mt):
        for ci in range(7):
            s, k = acquire()
            b = ci % 2
            for kc in range(NKC):
                mm(ps[b][:, :], pool_sb[:, s, kc * 128:(kc + 1) * 128], xT[:, kc, :], kc == 0, kc == NKC - 1,
                   [k, f"xT{kc}"], [f"ps{b}"])
            release()
            if ci < 4:
                cp("act", qT[:, ci, :], ps[b][:, :], [f"ps{b}"], ["qT"])
            elif ci < 6:
                cp("dve", qiT[:, ci - 4, :], ps[b][:, :], [f"ps{b}"], ["qiT"])
            else:
                cp("act", kiT[:, mt * TM:(mt + 1) * TM], ps[b][:, :], [f"ps{b}"], ["kiT"])
        for h in range(8):
            b = 2 + h % 2
            p0 = (h % 2) * 64
            mm(ps[b][:, :], wukT[p0:p0 + 64, h // 2, :], qT[p0:p0 + 64, h // 2, :], True, True,
               ["wukT", "qT"], [f"ps{b}"])
            if h % 2 == 0:
                E("act", lambda e, h=h, b=b: e.activation(out=qabsT[:, h, :], in_=ps[b][:, :], func=AF.Identity, scale=0.125),
                  [f"ps{b}"], ["qabsT"])
            else:
                ts("dve", qabsT[:, h, :], ps[b][:, :], 0.125, None, ALU.mult, None, [f"ps{b}"], ["qabsT"])

    def ptm_gen(mt):
        zs = []
        for kc in range(NKC):
            zs.append(acquire())
        sc_, kc_ = acquire()

        def zbanks(tb):
            return (4, 5) if tb % 2 == 0 else (2, 3)

        def mm_part(tb):
            tsl = slice(tb * 128, (tb + 1) * 128)
            zb = zbanks(tb)
            for half in range(2):
                b = zb[half]
                for kc in range(NKC):
                    s, k = zs[kc]
                    mm(ps[b][:, :], xT[:, kc, tsl], pool_sb[:, s, half * 512:(half + 1) * 512], kc == 0, kc == NKC - 1,
                       [k, f"xT{kc}"], [f"ps{b}"])
            for kc in range(NKC):
                mm(ps[6][:, 0:132], xT[:, kc, tsl], pool_sb[:, sc_, kc * 132:(kc + 1) * 132], kc == 0, kc == NKC - 1,
                   [kc_, f"xT{kc}"], ["ps6"])

        def elem_part(tb):
            blk = mt * NTB + tb
            zb = zbanks(tb)
            act(junk[:], ps[6][:, 0:128], AF.Square, ["ps6"], ["junk", "sm4"], accum_out=sm[:, 4:5])
            ts("dve", sm[:, 5:6], sm[:, 4:5], 1.0 / LAT, EPS, ALU.mult, ALU.add, ["sm4"], ["sm5"])
            act(sm[:, 6:7], sm[:, 5:6], AF.Sqrt, ["sm5"], ["sm6"])
            E("dve", lambda e: e.reciprocal(sm[:, 7:8], sm[:, 6:7]), ["sm6"], ["sm7"])
            stt(c_tok[:, blk, :], ps[6][:, 0:128], sm[:, 7:8], kvg_sb[:], ALU.mult, ALU.mult,
                ["ps6", "sm7", "kvg"], [f"c_tok{blk}"])
            cp("dve", w_tok[:, tb, :], ps[6][:, 128:132], ["ps6"], ["w_tok"])
            act(ug[:], ps[zb[0]][:, :], AF.Gelu_apprx_tanh, [f"ps{zb[0]}"], ["ug"])
            act(vg[:], ps[zb[1]][:, :], AF.Gelu_apprx_tanh, [f"ps{zb[1]}"], ["vg"])
            E("dve", lambda e: e.bn_stats(stats[:, 0, :], vg[:]), ["vg"], ["stats"])
            E("dve", lambda e: e.bn_aggr(mv[:], stats[:, 0, :]), ["stats"], ["mv"])
            ts("dve", sm[:, 8:9], mv[:, 1:2], EPS, None, ALU.add, None, ["mv"], ["sm8"])
            act(sm[:, 9:10], sm[:, 8:9], AF.Sqrt, ["sm8"], ["sm9"])
            E("dve", lambda e: e.reciprocal(sm[:, 10:11], sm[:, 9:10]), ["sm9"], ["sm10"])
            stt(sm[:, 11:12], mv[:, 0:1], -1.0, sm[:, 10:11], ALU.mult, ALU.mult, ["mv", "sm10"], ["sm11"])
            act(vg[:], vg[:], AF.Identity, ["vg", "sm10", "sm11"], ["vg"], bias=sm[:, 11:12], scale=sm[:, 10:11])
            tt("dve", vg[:], vg[:], glg_sb[:], ALU.mult, ["vg", "glg"], ["vg"])
            tt("pool", vn[:], vg[:], glb_sb[:], ALU.add, ["vg", "glb"], ["vn"])

        def spatial_part(tb):
            blk = mt * NTB + tb
            tsl = slice(tb * 128, (tb + 1) * 128)
            zb = zbanks(tb)
            tr(psb[7][:, 0:128], c_tok[:, blk, :], ident_b[:], [f"c_tok{blk}", "ident_b"], ["ps7"])
            cp("act", cT[:, blk * 128:(blk + 1) * 128], psb[7][:, 0:128], ["ps7"], ["cT"])
            for g in range(8):
                mm(ps[zb[0]][:, g * 64:(g + 1) * 64], wsT[:, g, :], vn[:, g * 64:(g + 1) * 64], True, True,
                   ["wsT", "vn"], [f"ps{zb[0]}"])
            tt("dve", vg[:].rearrange("p (g d) -> p g d", g=8), ps[zb[0]][:, :].rearrange("p (g d) -> p g d", g=8),
               bs_tok[:].unsqueeze(2).to_broadcast([128, 8, 64]), ALU.add, [f"ps{zb[0]}", "bs_tok"], ["vg"])
            tt("dve", gm_tok[:], vg[:], ug[:], ALU.mult, ["vg", "ug"], ["gm_tok"])
            for ch in range(4):
                tr(psb[7][:, 256 + ch * 128:256 + (ch + 1) * 128], gm_tok[:, ch * 128:(ch + 1) * 128], ident_b[:],
                   ["gm_tok", "ident_b"], ["ps7"])
            cp("act", gmT[:, :, tsl], psb[7][:, 256:768].rearrange("p (c t) -> p c t", c=4), ["ps7"], ["gmT"])

        mm_part(0)
        for tb in range(NTB):
            elem_part(tb)
            if tb + 1 < NTB:
                mm_part(tb + 1)
            yield ("elem", tb)
            spatial_part(tb)
            yield ("spatial", tb)
        for _ in range(NKC + 1):
            release()

    def indexer_a(mt, tb, bg=None, nsteps=0, deferred=None, pre=None):
        i = mt * NTB + tb
        n = 128 * (i + 1)
        tsl = slice(tb * 128, (tb + 1) * 128)
        nch = (n + 511) // 512
        for c5 in range(nch):
            k0 = c5 * 512
            kn = min(512, n - k0)
            for h in range(4):
                b = h % 2
                p0 = (h % 2) * 64
                mm(ps[b][:, 0:kn], qiT[p0:p0 + 64, h // 2, tsl], kiT[p0:p0 + 64, k0:k0 + kn], True, True,
                   ["qiT", "kiT"], [f"ps{b}"])
                act(rtmp[:, b, 0:kn], ps[b][:, 0:kn], AF.Relu, [f"ps{b}"], [f"rtmp{b}"])
                in1 = ramp[:, 0:kn] if h == 0 else scores[:, k0:k0 + kn]
                rk = ["ramp"] if h == 0 else []
                stt(scores[:, k0:k0 + kn], rtmp[:, b, 0:kn], w_tok[:, tb, h:h + 1], in1, ALU.mult, ALU.add,
                    [f"rtmp{b}", "w_tok"] + rk, ["arena"])
                if pre is not None:
                    next(pre, None)
            if c5 > 0:
                ts("dve", scores[:, k0:k0 + kn], scores[:, k0:k0 + kn], float(-TIE_EPS * k0), None, ALU.add, None,
                   [], ["arena"])
        tt("dve", scores[:, n - 128:n], scores[:, n - 128:n], trineg[:], ALU.add, ["trineg"], ["arena"])
        if pre is not None:
            for _ in pre:
                pass
        if deferred is not None:
            deferred()
        lo, mx, wd, av = (bis[:, 0:1], bis[:, 1:2], bis[:, 2:3], bis[:, 4:5])
        mk = maskA[:, 8192:8192 + n]
        if i < 2:
            E("dve", lambda e: e.memset(lo, -1.0e29), [], ["bis"])
        else:
            E("dve", lambda e: e.tensor_reduce(lo, scores[:, 0:n - 128], AX.X, ALU.min), ["arena"], ["bis"])
            E("dve", lambda e: e.tensor_reduce(mx, scores[:, 0:n], AX.X, ALU.max), ["arena"], ["bis"])
            tt("dve", wd, mx, lo, ALU.subtract, ["bis"], ["bis"])
            ts("dve", wd, wd, 1.0001, 1.0e-6, ALU.mult, ALU.add, ["bis"], ["bis"])
            ts("dve", wtab[:, 0:NIT + 1], p2t[:, 0:NIT + 1], wd, None, ALU.mult, None, ["bis", "p2t"], ["wtab"])
            ts("dve", w2tab[:, 0:NIT + 1], wtab[:, 0:NIT + 1], 2.0, None, ALU.mult, None, ["wtab"], ["w2tab"])
            tt("dve", midt[:, 0:1], lo, wtab[:, 0:1], ALU.add, ["bis", "wtab"], ["midt"])
            if bg is None:
                nsteps = 0
            r = -(-nsteps // NIT) if nsteps else 0
            n1 = ((224.0 + n) / 1.2 + 530.0 * r + 250.0 - 600.0) / (1.0 / 0.88 + 1.0 / 1.2)
            n1 = int(min(max(64, round(n1 / 64.0) * 64), n - 128))
            n2 = n - n1
            thr = float(TOPK) - 0.5 - 0.5 * n2
            pulled = 0
            E("dve", lambda e: e.memset(at[:, 0:1], 0.0), [], ["at"])
            E("dve", lambda e: e.tensor_copy(mmt[:, 0:1], midt[:, 0:1]), ["midt"], ["mmt"])
            for k in range(NIT):
                E("dve", lambda e, k=k: e.scalar_tensor_tensor(
                    mk[:, 0:n1], scores[:, 0:n1], mmt[:, k:k + 1], at[:, k:k + 1].to_broadcast([128, n1]),
                    ALU.subtract, ALU.is_ge, accum_out=cntt[:, k:k + 1]),
                  ["mmt", "at", "arena"], ["mkA", "cntt"])
                act(mk[:, n1:n], scores[:, n1:n], AF.Sign, ["midt", "arena"], ["mkB", "sgnt"],
                    bias=midt[:, k:k + 1], scale=-1.0, accum_out=sgnt[:, k:k + 1])
                if bg is not None:
                    for _ in range(r):
                        if pulled < nsteps - 1:
                            next(bg, None)
                            pulled += 1
                stt(mmt[:, k + 1:k + 2], at[:, k:k + 1], mmt[:, k:k + 1], wtab[:, k + 1:k + 2], ALU.add, ALU.subtract,
                    ["at", "mmt", "wtab"], ["mmt"])
                stt(c2t[:, k:k + 1], sgnt[:, k:k + 1], -0.5, cntt[:, k:k + 1], ALU.mult, ALU.add,
                    ["sgnt", "cntt"], ["c2t"])
                stt(at[:, k + 1:k + 2], c2t[:, k:k + 1], thr, w2tab[:, k + 1:k + 2], ALU.is_ge, ALU.mult,
                    ["c2t", "w2tab"], ["at"])
                act(midt[:, k + 1:k + 2], at[:, k + 1:k + 2], AF.Identity, ["mmt", "at"], ["midt"],
                    bias=mmt[:, k + 1:k + 2], scale=1.0)
            tt("dve", lo, midt[:, NIT:NIT + 1], wtab[:, NIT:NIT + 1], ALU.subtract, ["midt", "wtab"], ["bis"])
        ts("dve", mk, scores[:, 0:n], lo, None, ALU.is_ge, None, ["bis", "arena"], ["mkA", "mkB"])

    def idxb_gen(mt, tb, banks=(0, 1)):
        i = mt * NTB + tb
        par = tb % 2
        nb = i + 1
        for gi, j0 in enumerate(range(0, nb, 4)):
            jn = min(4, nb - j0)
            bb = banks[gi % 2]
            for jj in range(jn):
                j = j0 + jj
                tr(psb[bb][:, jj * 128:(jj + 1) * 128], maskA[:, 8192 + j * 128:8192 + (j + 1) * 128], ident_b[:],
                   ["mkA", "mkB", "ident_b"], [f"ps{bb}"])
            act(maskT[:, par, j0:j0 + jn, :], psb[bb][:, 0:jn * 128].rearrange("p (j t) -> p j t", j=jn),
                AF.Identity, [f"ps{bb}"], [f"maskT{par}"], bias=-30000.0, scale=30000.0)
            yield gi

    def indexer_b(mt, tb):
        for _ in idxb_gen(mt, tb):
            pass

    def attention_gen(mt, tb):
        i = mt * NTB + tb
        par = tb % 2
        tsl = slice(tb * 128, (tb + 1) * 128)
        steps = [(hg, j) for hg in range(2) for j in range(i + 1)]

        def emit_S(idx):
            hg, j = steps[idx]
            bs_ = 6 + (idx % 2)
            mm(ps[bs_][:, :].rearrange("p (h t) -> p h t", h=4), cT[:, j * 128:(j + 1) * 128],
               qabsT[:, hg * 4:(hg + 1) * 4, tsl], True, False, ["cT", "qabsT"], [f"ps{bs_}"])
            mm(ps[bs_][:, :].rearrange("p (h t) -> p h t", h=4), ident_b[:],
               maskT[:, par, j, :].unsqueeze(1).to_broadcast([128, 4, 128]), False, True,
               ["ident_b", f"maskT{par}"], [f"ps{bs_}"])

        emit_S(0)
        if len(steps) > 1:
            emit_S(1)
        for idx, (hg, j) in enumerate(steps):
            bo, bd = (4, 5) if hg == 0 else (2, 3)
            bs_ = 6 + (idx % 2)
            pt = idx % 3
            act(PT[:, pt, :], ps[bs_][:, :], AF.Exp, [f"ps{bs_}"], [f"PT{pt}"])
            if idx + 2 < len(steps):
                emit_S(idx + 2)
            mm(ps[bo][:, :], c_tok[:, j, :], PT[:, pt, :], j == 0, j == i, [f"c_tok{j}", f"PT{pt}"], [f"ps{bo}"])
            mm(ps[bd][:, :], ones_b[:], PT[:, pt, :], j == 0, j == i, ["ones_b", f"PT{pt}"], [f"ps{bd}"])
            yield idx
        for hg in range(2):
            bo, bd = (4, 5) if hg == 0 else (2, 3)
            act(lnd[:], ps[bd][:, :], AF.Ln, [f"ps{bd}"], ["lnd"])
            act(rden[:], lnd[:], AF.Exp, ["lnd"], ["rden"], scale=-1.0)
            tt("dve", olat[:, hg, :], ps[bo][:, :], rden[:], ALU.mult, [f"ps{bo}", "rden"], [f"olat{hg}"])
            for hp in range(2):
                for e2 in range(2):
                    hl = hp * 2 + e2
                    h = hg * 4 + hl
                    mm(ps[0][:, (hg * 2 + hp) * 128:(hg * 2 + hp + 1) * 128], wuvp[:, h, :],
                       olat[:, hg, hl * 128:(hl + 1) * 128], e2 == 0, e2 == 1, ["wuvp", f"olat{hg}"], ["ps0"])
        cp("act", attT[:, :, tsl], ps[0][:, :].rearrange("p (c t) -> p c t", c=4), ["ps0"], ["attT"])

    def out_proj_block(tb, ws_):
        tsl = slice(tb * 128, (tb + 1) * 128)
        for dh in range(2):
            b = 6 + dh
            for kc in range(NKC):
                s_, k = ws_[kc]
                src = attT[:, kc, tsl] if kc < 4 else gmT[:, kc - 4, tsl]
                mm(ps[b][:, :], src, pool_sb[:, s_, dh * 512:(dh + 1) * 512], kc == 0, kc == NKC - 1,
                   [k, "attT", "gmT"], [f"ps{b}"])
        layernorm_block(tb, [6, 7], x_res[:, tb, :], f"x_res{tb}")

    store_ops = []
    make_xT()
    for mt in range(NMT):
        if mt > 0:
            load_ln(0)
        ffn(0)
        layernorm_tile(lambda tb: [tb * 2, tb * 2 + 1], lambda tb: (x_res[:, tb, :], f"x_res{tb}"), make_xT_block)
        if mt == 0:
            emit_late_casts(12)
        load_ln(1)
        proj_feature_major(mt)
        pg = ptm_gen(mt)
        next(pg)
        ws_ = [acquire() for _ in range(NKC)]
        indexer_a(mt, 0, bg=pg, nsteps=2 * NTB)
        for _ in pg:
            pass
        if mt == 0:
            emit_late_casts(12)
        pending = None
        pre = idxb_gen(mt, 0, banks=(2, 3))
        for tb in range(NTB):
            if mt == 0:
                emit_late_casts(11 if tb < NTB - 1 else 99)
            g = attention_gen(mt, tb)
            if tb + 1 < NTB:
                indexer_a(mt, tb + 1, bg=g, nsteps=2 * (mt * NTB + tb + 1), deferred=pending, pre=pre)
                pre = None
            else:
                for _ in pre:
                    pass
                pre = None
                if pending is not None:
                    pending()
            pending = None
            for _ in g:
                pass
            if tb + 1 < NTB:
                pre = idxb_gen(mt, tb + 1, banks=(2, 3))
                pending = (lambda tb=tb, ws_=ws_: (out_proj_block(tb, ws_), make_xT_block(tb)))
            else:
                out_proj_block(tb, ws_)
                make_xT_block(tb)
        for _ in range(NKC):
            release()
        load_ln(2)
        ffn(1)

        def after3(tb, mt=mt):
            r0 = mt * TM + tb * 128
            op = dma("pool", out_d[r0:r0 + 128, :], obuf[:, tb % 2, :], [f"obuf{tb % 2}"], [], dsem=f"ost{tb % 2}")
            store_ops.append(op)
            if mt + 1 < NMT:
                load_x_block(mt + 1, tb)
                make_xT_block(tb)

        layernorm_tile(lambda tb: [tb * 2, tb * 2 + 1], lambda tb: (obuf[:, tb % 2, :], f"obuf{tb % 2}"), after3)
    E("pool", lambda e: e.memset(junk[:, 0:1], 0.0), [], ["junk"], extra=[store_ops[-1], store_ops[-2]])

    S.finalize()
    sem_names = set()
    for e in S.ENGS:
        for op in S.ops[e]:
            if op.dsem is not None:
                sem_names.add(op.dsem)
    sems = {n: es.enter_context(nc.semaphore("d_" + n)) for n in sorted(sem_names)}
    esem = {e: es.enter_context(nc.semaphore("e_" + e)) for e in S.ENGS}

    def replay(engname, eng):
        for op in S.ops[engname]:
            for d in op.waits:
                if d.dsem is not None:
                    eng.wait_ge(sems[d.dsem], d.dval)
                else:
                    eng.wait_ge(esem[d.eng], d.val)
            ins = op.fn(eng)
            if op.dsem is not None:
                ins.then_inc(sems[op.dsem], 16)
            elif op.awaited:
                ins.then_inc(esem[engname], 1)

    with nc.Block() as block:
        @block.sync
        def _(e):
            replay("sp", e)

        @block.scalar
        def _(e):
            replay("act", e)

        @block.vector
        def _(e):
            replay("dve", e)

        @block.gpsimd
        def _(e):
            replay("pool", e)

        @block.tensor
        def _(e):
            replay("pe", e)

    es.close()
    return nc


def _consts():
    ident = np.eye(128, dtype=np.float32)
    t = np.arange(128)[:, None]
    s = np.arange(128)[None, :]
    trineg = np.where(s <= t, 0.0, NEG).astype(np.float32)
    tril = (np.arange(128)[:, None] <= np.arange(128)[None, :]).astype(np.float32)
    ramp = (-TIE_EPS * np.arange(512, dtype=np.float64)).astype(np.float32)[None, :]
    pow2 = (2.0 ** -(np.arange(32, dtype=np.float64) + 1.0)).astype(np.float32)[None, :]
    return {"c_ident": ident, "c_trineg": trineg, "c_tril": tril, "c_ramp": ramp, "c_pow2": pow2}


_W_NAMES = ["ffn1_w1", "ffn1_w3", "ffn1_w2", "ln1_g", "ln1_b", "w_in", "kv_norm_g", "w_uk", "w_uv",
            "gmlp_ln_g", "gmlp_ln_b", "gmlp_ws", "gmlp_bs", "w_out", "ln2_g", "ln2_b",
            "ffn2_w1", "ffn2_w3", "ffn2_w2", "ln3_g", "ln3_b"]


def _prep_weights(inputs):
    m = {}
    for n in _W_NAMES:
        a = np.ascontiguousarray(np.asarray(inputs[n], dtype=np.float32)[0])
        if a.ndim == 1:
            a = a[None, :]
        m[n] = a
    m.update(_consts())
    return m


def kernel(**inputs):
    x = np.asarray(inputs["x"], dtype=np.float32)
    B, L, _ = x.shape
    nc = build_nc(L)
    wm = _prep_weights(inputs)
    in_maps = []
    for b in range(B):
        d = dict(wm)
        d["x"] = np.ascontiguousarray(x[b])
        in_maps.append(d)
    res = run_bass_kernel_spmd(nc, in_maps, core_ids=list(range(B)))
    return np.stack([np.asarray(r["out"], dtype=np.float32) for r in res.results], axis=0)
```
